# Optimizing a Trainium2 kernel written in Bass

```python
import jax, jax.numpy as jnp
from jax import lax
import numpy as np

D_MODEL = 2048
BATCH = 2
SEQ = 8192
DEPTH = 2

D_MIX = D_MODEL
D_A = D_MIX // 4
D_B = D_MIX // 4
D_C = D_MIX // 4
D_D = D_MIX // 4
GDN_HEAD_DIM = 128
GDN_HEADS = D_A // GDN_HEAD_DIM
GDN_CONV = 4
GDN_CHUNK = 64
GDN_CHUNK_LOG2 = 6
LRU_BLOCKS = 8
LRU_BLOCK_DIM = D_B // LRU_BLOCKS
LRU_CONV = 4
LRU_C = 8.0
SGU_GROUPS = 4
SGU_GROUP_DIM = D_C // SGU_GROUPS
SGU_CHUNK = 128
SCONV_WIDTH = 3
D_FF = (D_MODEL * 11) // 4
FFN_CONV = 3
IN_SIZES = (D_A, D_A, D_A, D_A, GDN_HEADS, GDN_HEADS, D_B, D_B, 2 * D_C, D_D, D_D, D_D)
N_IN = 4 * D_A + 2 * GDN_HEADS + 2 * D_B + 2 * D_C + 3 * D_D
EPS = 1e-6

kernel_name = "hybrid_parallel_head_groups_gdn_rglru_sgu_shortconv"


def _split(t, sizes):
    idx, acc = [], 0
    for s in sizes[:-1]:
        acc += s
        idx.append(acc)
    return jnp.split(t, idx, axis=-1)


def rms_norm(x, w):
    xf = x.astype(jnp.float32)
    y = xf * lax.rsqrt(jnp.mean(xf * xf, axis=-1, keepdims=True) + EPS)
    return (y * w.astype(jnp.float32)).astype(x.dtype)


def layer_norm(x, w, b):
    xf = x.astype(jnp.float32)
    mu = jnp.mean(xf, axis=-1, keepdims=True)
    xc = xf - mu
    y = xc * lax.rsqrt(jnp.mean(xc * xc, axis=-1, keepdims=True) + EPS)
    return y * w.astype(jnp.float32) + b.astype(jnp.float32)


def causal_dwconv(x, w, b=None):
    K, C = w.shape
    y = lax.conv_general_dilated(x, w[:, None, :].astype(x.dtype), window_strides=(1,),
                                 padding=[(K - 1, 0)], dimension_numbers=('NWC', 'WIO', 'NWC'),
                                 feature_group_count=C)
    if b is not None:
        y = y + b.astype(x.dtype)
    return y


def l2norm(t):
    return t * lax.rsqrt(jnp.sum(t * t, axis=-1, keepdims=True) + EPS)


def chunk_gated_delta_rule(q, k, v, g, beta):
    Bn, S, H, Dk = q.shape
    Dv = v.shape[-1]
    nC = S // GDN_CHUNK
    C = GDN_CHUNK
    q = l2norm(q) * (Dk ** -0.5)
    k = l2norm(k)

    def chunks(t):
        return t.reshape(Bn, nC, C, H, -1).transpose(0, 3, 1, 2, 4)

    qc, kc, vc = chunks(q), chunks(k), chunks(v)
    gc = g.reshape(Bn, nC, C, H).transpose(0, 3, 1, 2)
    bc = beta.reshape(Bn, nC, C, H).transpose(0, 3, 1, 2)
    gcum = jnp.cumsum(gc, axis=-1)
    causal = jnp.tril(jnp.ones((C, C), dtype=bool))
    strict = jnp.tril(jnp.ones((C, C), dtype=bool), -1)
    diff = gcum[..., :, None] - gcum[..., None, :]
    decay = jnp.where(causal, jnp.exp(jnp.where(causal, diff, 0.0)), 0.0)
    kb = kc * bc[..., None]
    M = jnp.where(strict, jnp.einsum('bhncd,bhnsd->bhncs', kb, kc) * decay, 0.0)
    N = -M
    T = jnp.eye(C, dtype=jnp.float32) + N
    P = N
    for _ in range(GDN_CHUNK_LOG2 - 1):
        P = jnp.einsum('bhnij,bhnjk->bhnik', P, P)
        T = T + jnp.einsum('bhnij,bhnjk->bhnik', T, P)
    w = jnp.einsum('bhncs,bhnsd->bhncd', T, kb * jnp.exp(gcum)[..., None])
    u = jnp.einsum('bhncs,bhnsd->bhncd', T, vc * bc[..., None])
    attn = jnp.where(causal, jnp.einsum('bhncd,bhnsd->bhncs', qc, kc) * decay, 0.0)
    q_g = qc * jnp.exp(gcum)[..., None]
    k_g = kc * jnp.exp(gcum[..., -1:] - gcum)[..., None]
    g_last = jnp.exp(gcum[..., -1])

    def step(state, inp):
        q_i, a_i, u_i, w_i, k_i, gl_i = inp
        v_new = u_i - jnp.einsum('bhck,bhkv->bhcv', w_i, state)
        o_i = jnp.einsum('bhck,bhkv->bhcv', q_i, state) + jnp.einsum('bhcs,bhsv->bhcv', a_i, v_new)
        state = state * gl_i[..., None, None] + jnp.einsum('bhck,bhcv->bhkv', k_i, v_new)
        return state, o_i

    xs = tuple(jnp.moveaxis(t, 2, 0) for t in (q_g, attn, u, w, k_g, g_last))
    s0 = jnp.zeros((Bn, H, Dk, Dv), jnp.float32)
    _, o = lax.scan(step, s0, xs)
    return o.transpose(1, 0, 3, 2, 4).reshape(Bn, S, H, Dv)


def gdn_mixer(q, k, v, z, b, a, conv_w, a_log, dt_bias, norm_w):
    Bn, S, _ = q.shape
    qkv = jax.nn.silu(causal_dwconv(jnp.concatenate([q, k, v], axis=-1), conv_w))
    q, k, v = jnp.split(qkv, 3, axis=-1)

    def heads(t):
        return t.reshape(Bn, S, GDN_HEADS, GDN_HEAD_DIM).astype(jnp.float32)

    beta = jax.nn.sigmoid(b.astype(jnp.float32))
    g = -jnp.exp(a_log.astype(jnp.float32)) * jax.nn.softplus(a.astype(jnp.float32) + dt_bias.astype(jnp.float32))
    o = chunk_gated_delta_rule(heads(q), heads(k), heads(v), g, beta)
    o = o * lax.rsqrt(jnp.mean(o * o, axis=-1, keepdims=True) + EPS) * norm_w.astype(jnp.float32) * jax.nn.silu(heads(z))
    return o.reshape(Bn, S, D_A).astype(z.dtype)


def lru_scan(a, b):
    def combine(l, r):
        return (l[0] * r[0], r[0] * l[1] + r[1])
    _, h = lax.associative_scan(combine, (a, b), axis=1)
    return h


def rglru_mixer(xb, gate, conv_w, conv_b, wa, ba, wx, bx, lam):
    Bn, S, _ = xb.shape
    xc = causal_dwconv(xb, conv_w, conv_b).astype(jnp.float32)
    xblk = xc.reshape(Bn, S, LRU_BLOCKS, LRU_BLOCK_DIM)
    r = jax.nn.sigmoid(jnp.einsum('bshi,hij->bshj', xblk, wa.astype(jnp.float32)) + ba.astype(jnp.float32)).reshape(Bn, S, D_B)
    i = jax.nn.sigmoid(jnp.einsum('bshi,hij->bshj', xblk, wx.astype(jnp.float32)) + bx.astype(jnp.float32)).reshape(Bn, S, D_B)
    log_a = -LRU_C * r * jax.nn.softplus(-lam.astype(jnp.float32))
    a = jnp.exp(log_a)
    mult = jnp.sqrt(-jnp.expm1(2.0 * log_a))
    h = lru_scan(a, mult * (i * xc))
    return (h * jax.nn.gelu(gate.astype(jnp.float32))).astype(xb.dtype)


def sgu_mixer(uv, ln_w, ln_b, ws, bs):
    Bn, S, _ = uv.shape
    uvf = jax.nn.gelu(uv.astype(jnp.float32))
    u, v = jnp.split(uvf, 2, axis=-1)
    v = layer_norm(v, ln_w, ln_b)
    v = v.reshape(Bn, S // SGU_CHUNK, SGU_CHUNK, SGU_GROUPS, SGU_GROUP_DIM)
    mask = jnp.tril(jnp.ones((SGU_CHUNK, SGU_CHUNK), dtype=bool))
    wsm = jnp.where(mask, ws.astype(jnp.float32), 0.0)
    v = jnp.einsum('gts,bnsgd->bntgd', wsm, v) + bs.astype(jnp.float32).T[:, :, None]
    return (u * v.reshape(Bn, S, D_C)).astype(uv.dtype)


def short_conv_mixer(bg, cg, hh, conv_w):
    return bg * causal_dwconv(cg * hh, conv_w)


def conv_ffn(h, up, conv_w, conv_b, down):
    hid = causal_dwconv(h @ up.astype(h.dtype), conv_w, conv_b)
    gate, val = jnp.split(hid, 2, axis=-1)
    return (jax.nn.gelu(gate) * val) @ down.astype(h.dtype)


def setup_inputs(seed: int = 0) -> dict:
    key = jax.random.key(seed)
    ks = iter(jax.random.split(key, 48))
    L = DEPTH
    f32 = jnp.float32

    def nrm(shape, scale):
        return jax.random.normal(next(ks), shape, f32) * scale

    def gain(shape):
        return 1.0 + 0.1 * jax.random.normal(next(ks), shape, f32)

    def unif(shape, lo, hi):
        return jax.random.uniform(next(ks), shape, f32, lo, hi)

    s = unif((L, D_B), 0.9, 0.999) ** (1.0 / LRU_C)
    lru_lambda = jnp.log(s) - jnp.log1p(-s)
    dt = jnp.exp(unif((L, GDN_HEADS), float(np.log(1e-3)), float(np.log(1e-1))))
    gdn_dt_bias = dt + jnp.log(-jnp.expm1(-dt))
    return {
        "x": nrm((BATCH, SEQ, D_MODEL), 1.0),
        "pre_mix_norm": gain((L, D_MODEL)),
        "w_in": nrm((L, D_MODEL, N_IN), D_MODEL ** -0.5),
        "gdn_conv_w": nrm((L, GDN_CONV, 3 * D_A), GDN_CONV ** -0.5),
        "gdn_a_log": jnp.log(unif((L, GDN_HEADS), 1.0, 16.0)),
        "gdn_dt_bias": gdn_dt_bias,
        "gdn_norm_w": gain((L, GDN_HEAD_DIM)),
        "lru_conv_w": nrm((L, LRU_CONV, D_B), LRU_CONV ** -0.5),
        "lru_conv_b": nrm((L, D_B), 0.01),
        "lru_wa": nrm((L, LRU_BLOCKS, LRU_BLOCK_DIM, LRU_BLOCK_DIM), LRU_BLOCK_DIM ** -0.5),
        "lru_ba": nrm((L, LRU_BLOCKS, LRU_BLOCK_DIM), 0.01),
        "lru_wx": nrm((L, LRU_BLOCKS, LRU_BLOCK_DIM, LRU_BLOCK_DIM), LRU_BLOCK_DIM ** -0.5),
        "lru_bx": nrm((L, LRU_BLOCKS, LRU_BLOCK_DIM), 0.01),
        "lru_lambda": lru_lambda,
        "sgu_ln_w": gain((L, D_C)),
        "sgu_ln_b": nrm((L, D_C), 0.01),
        "sgu_ws": nrm((L, SGU_GROUPS, SGU_CHUNK, SGU_CHUNK), SGU_CHUNK ** -0.5),
        "sgu_b": gain((L, SGU_GROUPS, SGU_CHUNK)),
        "sconv_w": nrm((L, SCONV_WIDTH, D_D), SCONV_WIDTH ** -0.5),
        "grp_norm_w": gain((L, 3, D_B)),
        "w_out": nrm((L, D_MIX, D_MODEL), D_MIX ** -0.5),
        "post_mix_norm": gain((L, D_MODEL)),
        "pre_ffn_norm": gain((L, D_MODEL)),
        "ffn_up": nrm((L, D_MODEL, 2 * D_FF), D_MODEL ** -0.5),
        "ffn_conv_w": nrm((L, FFN_CONV, 2 * D_FF), FFN_CONV ** -0.5),
        "ffn_conv_b": nrm((L, 2 * D_FF), 0.01),
        "ffn_down": nrm((L, D_FF, D_MODEL), D_FF ** -0.5),
        "post_ffn_norm": gain((L, D_MODEL)),
    }


def reference(x, pre_mix_norm, w_in, gdn_conv_w, gdn_a_log, gdn_dt_bias, gdn_norm_w,
              lru_conv_w, lru_conv_b, lru_wa, lru_ba, lru_wx, lru_bx, lru_lambda,
              sgu_ln_w, sgu_ln_b, sgu_ws, sgu_b, sconv_w, grp_norm_w, w_out,
              post_mix_norm, pre_ffn_norm, ffn_up, ffn_conv_w, ffn_conv_b, ffn_down,
              post_ffn_norm):
    for l in range(DEPTH):
        h = rms_norm(x, pre_mix_norm[l])
        p = h @ w_in[l].astype(h.dtype)
        (q, k, v, z, b_gdn, a_gdn, lru_x, lru_gate, sgu_uv, sc_b, sc_c, sc_h) = _split(p, IN_SIZES)
        y_a = gdn_mixer(q, k, v, z, b_gdn, a_gdn, gdn_conv_w[l], gdn_a_log[l], gdn_dt_bias[l], gdn_norm_w[l])
        y_b = rms_norm(rglru_mixer(lru_x, lru_gate, lru_conv_w[l], lru_conv_b[l], lru_wa[l], lru_ba[l],
                                   lru_wx[l], lru_bx[l], lru_lambda[l]), grp_norm_w[l, 0])
        y_c = rms_norm(sgu_mixer(sgu_uv, sgu_ln_w[l], sgu_ln_b[l], sgu_ws[l], sgu_b[l]), grp_norm_w[l, 1])
        y_d = rms_norm(short_conv_mixer(sc_b, sc_c, sc_h, sconv_w[l]), grp_norm_w[l, 2])
        y = jnp.concatenate([y_a, y_b, y_c, y_d], axis=-1) @ w_out[l].astype(h.dtype)
        x = x + rms_norm(y, post_mix_norm[l])
        h = rms_norm(x, pre_ffn_norm[l])
        y = conv_ffn(h, ffn_up[l], ffn_conv_w[l], ffn_conv_b[l], ffn_down[l])
        x = x + rms_norm(y, post_ffn_norm[l])
    return x
```

```python
import numpy as np
from contextlib import ExitStack
import concourse.bass as bass
import concourse.mybir as mybir
from concourse.bass_utils import run_bass_kernel_spmd

F32 = mybir.dt.float32
BF16 = mybir.dt.bfloat16
AF = mybir.ActivationFunctionType
ALU = mybir.AluOpType

EPS = 1e-6
DM = 2048
NCORES = 8
SAME_ENGINE_SYNC = True
N_DMA_SLOTS = 16


_ALIAS = {}
B_PARTS = "shclg"
SCAN_W = 128


def _key(ap):
    n = ap.name
    return (_ALIAS.get(n, n),)


class KB:
    ENGS = ("pe", "act", "dve", "pool", "sp")

    def __init__(self, nc):
        self.nc = nc
        self.es = ExitStack()
        self.items = {e: [] for e in self.ENGS}
        self.seq = {e: 0 for e in self.ENGS}
        self.sem = {}
        for e in ("pe", "act", "dve", "pool"):
            self.sem[e] = self.es.enter_context(nc.semaphore("s_" + e))
        self.dsem = [self.es.enter_context(nc.semaphore("d%d" % i)) for i in range(N_DMA_SLOTS)]
        self.dcount = [0] * N_DMA_SLOTS
        self.dnext = 0
        self.dnextq = [0, 0]
        self.last_w = {}
        self.readers = {}
        self.subs = {}
        self.waited = {e: {} for e in self.ENGS}
        self.ntile = 0

    prefix = ""

    def sb(self, shape, dt=F32, name=None):
        self.ntile += 1
        base = name or ("t%d" % self.ntile)
        _ALIAS[self.prefix + base] = base
        return self.es.enter_context(self.nc.sbuf_tensor(self.prefix + base, list(shape), dt))

    def ps(self, shape=(128, 512), dt=F32, name=None):
        self.ntile += 1
        base = name or ("p%d" % self.ntile)
        _ALIAS[self.prefix + base] = base
        return self.es.enter_context(self.nc.psum_tensor(self.prefix + base, list(shape), dt))

    def _expand(self, key):
        name = key[0]
        if len(key) == 1:
            return [key] + [(name, s) for s in self.subs.get(name, ())]
        self.subs.setdefault(name, set()).add(key[1])
        return [key, (name,)]

    def _need(self, eng, tok, waits):
        if tok is None:
            return
        if tok[0] == "c" and tok[1] == eng:
            if eng in ("pe", "sp") or not SAME_ENGINE_SYNC:
                return
        key = tok[:2]
        if self.waited[eng].get(key, 0) >= tok[2]:
            return
        self.waited[eng][key] = tok[2]
        waits.append(tok)

    def _deps(self, eng, reads, writes):
        waits = []
        for r in reads:
            for kk in self._expand(r):
                self._need(eng, self.last_w.get(kk), waits)
        for w in writes:
            for kk in self._expand(w):
                self._need(eng, self.last_w.get(kk), waits)
                for tok in self.readers.get(kk, ()):
                    self._need(eng, tok, waits)
        return waits

    def _commit(self, tok, reads, writes):
        for r in reads:
            lst = self.readers.setdefault(r, [])
            lst[:] = [t for t in lst if t[:2] != tok[:2]]
            lst.append(tok)
        for w in writes:
            self.last_w[w] = tok
            self.readers[w] = []
            if len(w) == 1:
                for s in self.subs.get(w[0], ()):
                    self.last_w[(w[0], s)] = tok
                    self.readers[(w[0], s)] = []

    def op(self, eng, fn, reads=(), writes=()):
        reads = list(reads); writes = list(writes)
        waits = self._deps(eng, reads, writes)
        self.seq[eng] += 1
        tok = ("c", eng, self.seq[eng])
        self.items[eng].append((waits, fn, ("c", eng)))
        self._commit(tok, reads, writes)

    def dma(self, out, in_, eng="sp", rk=None, wk=None):
        reads = rk if rk is not None else ([_key(in_)] if in_.name in self.sbnames else [])
        writes = wk if wk is not None else ([_key(out)] if out.name in self.sbnames else [])
        half = N_DMA_SLOTS // 2
        qi = 0 if eng == "sp" else 1
        slot = qi * half + self.dnextq[qi]
        self.dnextq[qi] = (self.dnextq[qi] + 1) % half
        waits = self._deps(eng, reads, writes)
        if self.dcount[slot] > 0:
            self._need(eng, ("d", slot, self.dcount[slot]), waits)
        self.dcount[slot] += 16
        tok = ("d", slot, self.dcount[slot])
        self.items[eng].append((waits, (lambda e: e.dma_start(out=out, in_=in_)), ("d", slot)))
        self._commit(tok, reads, writes)

    def cc(self, fn, reads=(), writes=()):
        if not hasattr(self, "ccsem"):
            self.ccsem = self.es.enter_context(self.nc.semaphore("cc_sem_kb"))
            self.cccount = 0
        reads = list(reads); writes = list(writes)
        waits = self._deps("pool", reads, writes)
        if self.cccount > 0:
            self._need("pool", ("x", 0, self.cccount), waits)
        self.cccount += 1
        tok = ("x", 0, self.cccount)
        self.items["pool"].append((waits, fn, ("x", 0)))
        self._commit(tok, reads, writes)

    sbnames = None

    def act(self, out, in_, func, bias=None, scale=None, rk=None, wk=None):
        reads = list(rk) if rk is not None else [_key(in_)]
        kw = {}
        if bias is not None:
            kw["bias"] = bias
            if not isinstance(bias, (int, float)):
                reads.append(_key(bias))
        if scale is not None:
            kw["scale"] = scale
            if not isinstance(scale, (int, float)):
                reads.append(_key(scale))
        self.op("act", lambda e: e.activation(out=out, in_=in_, func=func, **kw), reads, wk if wk is not None else [_key(out)])

    def mm(self, out, lhsT, rhs, start=True, stop=True, rk=None, wk=None):
        self.op("pe", lambda e: e.matmul(out, lhsT=lhsT, rhs=rhs, start=start, stop=stop),
                rk if rk is not None else [_key(lhsT), _key(rhs)], wk if wk is not None else [_key(out)])

    def tt(self, out, in0, in1, op, eng="dve", rk=None, wk=None):
        self.op(eng, lambda e: e.tensor_tensor(out=out, in0=in0, in1=in1, op=op),
                rk if rk is not None else [_key(in0), _key(in1)], wk if wk is not None else [_key(out)])

    def ts(self, out, in0, s1, op0, s2=None, op1=None, eng="dve", rk=None, wk=None):
        reads = list(rk) if rk is not None else [_key(in0)]
        for s in (s1, s2):
            if s is not None and not isinstance(s, (int, float)):
                reads.append(_key(s))
        if op1 is None:
            fn = lambda e: e.tensor_scalar(out=out, in0=in0, scalar1=s1, scalar2=None, op0=op0)
        else:
            fn = lambda e: e.tensor_scalar(out=out, in0=in0, scalar1=s1, scalar2=s2, op0=op0, op1=op1)
        self.op(eng, fn, reads, wk if wk is not None else [_key(out)])

    def stt(self, out, in0, scalar, in1, op0, op1, rk=None, wk=None):
        reads = list(rk) if rk is not None else [_key(in0), _key(in1)]
        if not isinstance(scalar, (int, float)):
            reads.append(_key(scalar))
        self.op("dve", lambda e: e.scalar_tensor_tensor(out=out, in0=in0, scalar=scalar, in1=in1, op0=op0, op1=op1),
                reads, wk if wk is not None else [_key(out)])

    def copy(self, out, in_, eng="dve", rk=None, wk=None):
        self.op(eng, lambda e: e.tensor_copy(out=out, in_=in_), rk if rk is not None else [_key(in_)],
                wk if wk is not None else [_key(out)])

    def memset(self, ap, val, eng="pool"):
        self.op(eng, lambda e: e.memset(ap, val), [], [_key(ap)])

    def fillreg(self, e, val):
        if not hasattr(self, "_regs"):
            self._regs = {}
        if val not in self._regs:
            self._regs[val] = e.to_reg(val)
        return self._regs[val]

    def scope(self):
        kb = self

        class _S:
            def __enter__(s_):
                s_.outer = kb.es
                kb.es = ExitStack()
                return kb

            def __exit__(s_, *a):
                kb.es.close()
                kb.es = s_.outer
                return False
        return _S()

    def barrier(self):
        toks = [("c", e, self.seq[e]) for e in ("pe", "act", "dve", "pool") if self.seq[e]]
        toks += [("d", s_, self.dcount[s_]) for s_ in range(N_DMA_SLOTS) if self.dcount[s_]]
        for eng in self.ENGS:
            waits = []
            for tok in toks:
                self._need(eng, tok, waits)
            if waits:
                self.items[eng].append((waits, None, None))

    def finish(self):
        waits = []
        for slot in range(N_DMA_SLOTS):
            if self.dcount[slot]:
                self._need("sp", ("d", slot, self.dcount[slot]), waits)
        for e in ("pe", "act", "dve", "pool"):
            if self.seq[e]:
                self._need("sp", ("c", e, self.seq[e]), waits)
        if getattr(self, "cccount", 0):
            self._need("sp", ("x", 0, self.cccount), waits)
        self.items["sp"].append((waits, None, None))

    def build(self):
        nc = self.nc
        block = self.es.enter_context(nc.Block())

        def run(engname):
            def f(eng):
                for waits, fn, kind in self.items[engname]:
                    for tok in waits:
                        if tok[0] == "c":
                            eng.wait_ge(self.sem[tok[1]], tok[2])
                        elif tok[0] == "x":
                            eng.wait_ge(self.ccsem, tok[2])
                        else:
                            eng.wait_ge(self.dsem[tok[1]], tok[2])
                    if fn is None:
                        continue
                    ins = fn(eng)
                    if kind[0] == "c":
                        ins.then_inc(self.sem[kind[1]], 1)
                    elif kind[0] == "x":
                        ins.then_inc(self.ccsem, 1)
                    else:
                        ins.then_inc(self.dsem[kind[1]], 16)
            return f

        block.tensor(run("pe"))
        block.scalar(run("act"))
        block.vector(run("dve"))
        block.gpsimd(run("pool"))
        block.sync(run("sp"))
        self.es.close()


class Ctx:
    def __init__(self, nc):
        self.nc = nc
        self.k = KB(nc)
        k = self.k
        k.sbnames = set()
        _sb = k.sb

        def sb(shape, dt=F32, name=None):
            t = _sb(shape, dt, name)
            k.sbnames.add(t[:].name)
            return t
        k.sb = sb
        _ps = k.ps

        def ps(shape=(128, 512), dt=F32, name=None):
            t = _ps(shape, dt, name)
            k.sbnames.add(t[:].name)
            return t
        k.ps = ps
        self.ones = k.sb([128, 128], BF16, "ones")
        k.memset(self.ones[:], 1.0)
        self.sq = [k.sb([128, 512], BF16, "sq%d" % i) for i in range(2)]
        self.sqi = 0
        self.rstd = k.sb([128, 512], F32, "rstd")
        self.ps_stat = k.ps(name="ps_stat")

    def dram(self, name, shape, dt=F32, kind="ExternalInput"):
        return self.nc.dram_tensor(name, list(shape), dt, kind=kind).ap()

    def rstd_of(self, chunks, T, D, out=None, ps=None):
        k = self.k
        out = out if out is not None else self.rstd
        ps = ps if ps is not None else self.ps_stat
        n = len(chunks)
        for i, c in enumerate(chunks):
            s = self.sq[self.sqi]; self.sqi ^= 1
            k.act(s[:, :T], c, AF.Square)
            k.mm(ps[:, :T], self.ones[:], s[:, :T], start=(i == 0), stop=(i == n - 1))
        k.act(out[:, :T], ps[:, :T], AF.Sqrt, bias=EPS, scale=1.0 / D)
        k.op("dve", lambda e: e.reciprocal(out=out[:, :T], in_=out[:, :T]), [_key(out[:])], [_key(out[:])])
        return out


def emit_A(c, n_tok, xT, gam, w, wba, pT, pba):
    k = c.k
    T = min(512, n_tok)
    SBT = min(2048, n_tok)
    nsb = n_tok // SBT; nt = SBT // T
    gam_sb = k.sb([128, 16], F32, "gam_sb")
    k.dma(gam_sb[:], gam)
    wba_sb = k.sb([128, 16, 8], BF16, "wba_sb")
    k.dma(wba_sb[:], wba, eng="pool")
    xt = [k.sb([128, 16, T], F32, "xt%d" % i) for i in range(2)]
    xn = k.sb([128, 16, SBT], BF16, "xn")
    wsl = [k.sb([128, 16, 512], BF16, "wsl%d" % i) for i in range(2)]
    osb = [k.sb([128, T], F32, "osb%d" % i) for i in range(4)]
    psb = [k.ps(name="psA%d" % i) for i in range(4)]
    cnt = 0; xi = 0; wi = 0
    for sb_ in range(nsb):
        base = sb_ * SBT
        for t in range(nt):
            x_ = xt[xi % 2]; xi += 1
            k.dma(x_[:], xT[:, :, base + t * T:base + (t + 1) * T])
            r = c.rstd_of([x_[:, kk, :] for kk in range(16)], T, DM)
            for kk in range(16):
                k.stt(xn[:, kk, t * T:(t + 1) * T], x_[:, kk, :], gam_sb[:, kk:kk + 1], r[:, :T], ALU.mult, ALU.mult,
                      wk=[("xn", t)])
        for s in range(11):
            ws = wsl[wi % 2]; wi += 1
            k.dma(ws[:], w[s], eng="pool")
            for t in range(nt):
                for cc in range(4):
                    ch = s * 4 + cc
                    ps = psb[cnt % 4]; ob = osb[cnt % 4]
                    for kk in range(16):
                        k.mm(ps[:, :T], ws[:, kk, cc * 128:(cc + 1) * 128], xn[:, kk, t * T:(t + 1) * T],
                             start=(kk == 0), stop=(kk == 15), rk=[_key(ws[:]), ("xn", t)])
                    if cnt % 2 == 0:
                        k.act(ob[:], ps[:, :T], AF.Copy)
                    else:
                        k.copy(ob[:], ps[:, :T])
                    k.dma(pT(ch)[:, base + t * T:base + (t + 1) * T], ob[:])
                    cnt += 1
        for t in range(nt):
            ps = psb[cnt % 4]; ob = osb[cnt % 4]
            for kk in range(16):
                k.mm(ps[0:8, :T], wba_sb[:, kk, :], xn[:, kk, t * T:(t + 1) * T], start=(kk == 0), stop=(kk == 15),
                     rk=[_key(wba_sb[:]), ("xn", t)])
            k.act(ob[0:8, :], ps[0:8, :T], AF.Copy)
            k.dma(pba[:, base + t * T:base + (t + 1) * T], ob[0:8, :])
            cnt += 1


def emit_C(c, n_tok, xsrc, ysrc, xdst, vecs, wout, wup, cw, wdn):
    k = c.k
    T = min(512, n_tok); nt = n_tok // T
    vec_sb = k.sb([128, 60], F32, "vec_sb"); k.dma(vec_sb[:], vecs)
    cw_sb = k.sb([128, 88, 4], F32, "cw_sb"); k.dma(cw_sb[:], cw)
    GNW, PMN, PFN, PFFN = 0, 12, 28, 44
    xt = k.sb([128, 16, T], F32, "xt")
    yo = k.sb([128, 16, T], F32, "yo")
    nb = k.sb([128, 16, T], BF16, "nb")
    actb = k.sb([128, 44, T], BF16, "actb")
    wsl = [k.sb([128, 16, 256], BF16, "wsl%d" % i) for i in range(2)]
    wdl = [k.sb([128, 44, 128], BF16, "wdl%d" % i) for i in range(2)]
    hb = [k.sb([128, T + 2], F32, "hb%d" % i) for i in range(4)]
    cv = [k.sb([128, T], F32, "cv%d" % i) for i in range(4)]
    gl = [k.sb([128, T], F32, "gl%d" % i) for i in range(2)]
    halo = k.sb([128, 88, 2], F32, "halo")
    k.memset(halo[:], 0.0)
    psb = [k.ps(name="psC%d" % i) for i in range(4)]
    st = {"ps": 0, "w": 0, "wd": 0, "hb": 0}

    def tile(col0, Tt, halo_only, ocol0):
        k.dma(xt[:, :, :Tt], xsrc(col0, Tt))
        for g in range(4):
            stg = yo[:, 8 + 4 * (g % 2):12 + 4 * (g % 2), :Tt]
            skeys = [("yo", 8 + 4 * (g % 2) + i) for i in range(4)]
            k.dma(stg, ysrc(g, col0, Tt), wk=skeys)
            if g == 0:
                for cc in range(4):
                    k.act(nb[:, cc, :Tt], stg[:, cc, :], AF.Copy, rk=[skeys[cc]], wk=[("nb", cc)])
            else:
                for i in range(4):
                    s = c.sq[c.sqi]; c.sqi ^= 1
                    k.act(s[:, :Tt], stg[:, i, :], AF.Square, rk=[skeys[i]])
                    k.mm(c.ps_stat[:, :Tt], c.ones[:], s[:, :Tt], start=(i == 0), stop=(i == 3))
                k.act(c.rstd[:, :Tt], c.ps_stat[:, :Tt], AF.Sqrt, bias=EPS, scale=1.0 / 512)
                k.op("dve", lambda e: e.reciprocal(out=c.rstd[:, :Tt], in_=c.rstd[:, :Tt]), [("rstd",)], [("rstd",)])
                for cc in range(4):
                    col = GNW + (g - 1) * 4 + cc
                    k.stt(nb[:, 4 * g + cc, :Tt], stg[:, cc, :], vec_sb[:, col:col + 1], c.rstd[:, :Tt], ALU.mult, ALU.mult,
                          rk=[skeys[cc], ("rstd",)], wk=[("nb", 4 * g + cc)])
        for s in range(8):
            ws = wsl[st["w"] % 2]; st["w"] += 1
            k.dma(ws[:], wout[s], eng="pool")
            for cc in range(2):
                ch = 2 * s + cc
                ps = psb[st["ps"] % 4]; st["ps"] += 1
                for kk in range(16):
                    k.mm(ps[:, :Tt], ws[:, kk, cc * 128:(cc + 1) * 128], nb[:, kk, :Tt], start=(kk == 0), stop=(kk == 15))
                k.act(yo[:, ch, :Tt], ps[:, :Tt], AF.Copy, wk=[("yo", ch)])
        r = c.rstd_of([yo[:, kk, :Tt] for kk in range(16)], Tt, DM)
        for kk in range(16):
            k.stt(yo[:, kk, :Tt], yo[:, kk, :Tt], vec_sb[:, PMN + kk:PMN + kk + 1], r[:, :Tt], ALU.mult, ALU.mult,
                  rk=[("yo", kk), ("rstd",)], wk=[("yo", kk)])
            k.tt(xt[:, kk, :Tt], xt[:, kk, :Tt], yo[:, kk, :Tt], ALU.add, rk=[("xt", kk), ("yo", kk)], wk=[("xt", kk)])
        r = c.rstd_of([xt[:, kk, :Tt] for kk in range(16)], Tt, DM)
        for kk in range(16):
            k.stt(nb[:, kk, :Tt], xt[:, kk, :Tt], vec_sb[:, PFN + kk:PFN + kk + 1], r[:, :Tt], ALU.mult, ALU.mult,
                  rk=[("xt", kk), ("rstd",)], wk=[("nb", kk)])
        for j in range(44):
            ws = wsl[st["w"] % 2]; st["w"] += 1
            k.dma(ws[:], wup[j], eng="pool")
            cvs = []
            for gv in range(2):
                idx = 2 * j + gv
                ps = psb[st["ps"] % 4]; st["ps"] += 1
                hbt = hb[st["hb"] % 4]; cvt = cv[st["hb"] % 4]; st["hb"] += 1
                for kk in range(16):
                    k.mm(ps[:, :Tt], ws[:, kk, gv * 128:(gv + 1) * 128], nb[:, kk, :Tt], start=(kk == 0), stop=(kk == 15))
                k.copy(hbt[:, 0:2], halo[:, idx, :], eng="pool", rk=[("halo", idx)])
                k.act(hbt[:, 2:2 + Tt], ps[:, :Tt], AF.Copy)
                k.copy(halo[:, idx, :], hbt[:, Tt:Tt + 2], eng="pool", wk=[("halo", idx)])
                if halo_only:
                    continue
                k.ts(cvt[:, :Tt], hbt[:, 2:2 + Tt], cw_sb[:, idx, 2:3], ALU.mult, cw_sb[:, idx, 3:4], ALU.add)
                k.stt(cvt[:, :Tt], hbt[:, 1:1 + Tt], cw_sb[:, idx, 1:2], cvt[:, :Tt], ALU.mult, ALU.add)
                k.stt(cvt[:, :Tt], hbt[:, 0:Tt], cw_sb[:, idx, 0:1], cvt[:, :Tt], ALU.mult, ALU.add)
                cvs.append(cvt)
            if halo_only:
                continue
            g_ = gl[j % 2]
            k.act(g_[:, :Tt], cvs[0][:, :Tt], AF.Gelu_apprx_tanh)
            k.tt(actb[:, j, :Tt], g_[:, :Tt], cvs[1][:, :Tt], ALU.mult, wk=[("actb", j)])
        if halo_only:
            return
        for ch in range(16):
            wd = wdl[st["wd"] % 2]; st["wd"] += 1
            k.dma(wd[:], wdn[ch], eng="pool")
            ps = psb[st["ps"] % 4]; st["ps"] += 1
            for kk in range(44):
                k.mm(ps[:, :Tt], wd[:, kk, :], actb[:, kk, :Tt], start=(kk == 0), stop=(kk == 43))
            k.act(yo[:, ch, :Tt], ps[:, :Tt], AF.Copy, wk=[("yo", ch)])
        r = c.rstd_of([yo[:, kk, :Tt] for kk in range(16)], Tt, DM)
        for kk in range(16):
            k.stt(yo[:, kk, :Tt], yo[:, kk, :Tt], vec_sb[:, PFFN + kk:PFFN + kk + 1], r[:, :Tt], ALU.mult, ALU.mult,
                  rk=[("yo", kk), ("rstd",)], wk=[("yo", kk)])
            k.tt(xt[:, kk, :Tt], xt[:, kk, :Tt], yo[:, kk, :Tt], ALU.add, rk=[("xt", kk), ("yo", kk)], wk=[("xt", kk)])
        k.dma(xdst(ocol0, Tt), xt[:, :, :Tt])

    for t in range(nt):
        tile(t * T, T, False, t * T)


def emit_B(c, L, PTf, PBAf, uvsrc, Yf, gpar, lpar, lw, spar, swT, sbs, scw):
    k = c.k
    nC = L // 64
    assert nC <= 128
    base_prefix = k.prefix
    NB = 6
    psb = [k.ps(name="psB%d" % i) for i in range(NB)]
    psO = k.ps(name="psO")
    st = {"ps": 0}

    def nps():
        p = psb[st["ps"] % NB]; st["ps"] += 1
        return p

    ones32 = k.sb([128, 128], F32, "ones32"); k.memset(ones32[:], 1.0)
    ident32 = k.sb([128, 128], F32, "ident32")
    k.op("pool", lambda e: e.affine_select(out=ident32[:], in_=ones32[:], pattern=[[-1, 128]], compare_op=ALU.is_equal,
                                           fill=k.fillreg(e, 0.0), base=0, channel_multiplier=1), [("ones32",)], [("ident32",)])
    identb = k.sb([128, 128], BF16, "identb"); k.copy(identb[:], ident32[:])
    ones3 = k.sb([64, 8, 64], F32, "ones3"); k.memset(ones3[:], 1.0)
    identrep = k.sb([64, 8, 64], F32, "identrep")
    k.op("pool", lambda e: e.affine_select(out=identrep[:], in_=ones3[:], pattern=[[0, 8], [-1, 64]], compare_op=ALU.is_equal,
                                           fill=k.fillreg(e, 0.0), base=0, channel_multiplier=1), [("ones3",)], [("identrep",)])


    def sgu_part():
        Tq = min(512, L); nq = Tq // 128
        sp_ = k.sb([128, 8], F32, "sp_"); k.dma(sp_[:], spar)
        wsm = k.sb([128, 4, 128], F32, "wsm"); k.dma(wsm[:], swT)
        k.op("pool", lambda e: e.affine_select(out=wsm[:], in_=wsm[:], pattern=[[0, 4], [1, 128]], compare_op=ALU.is_ge,
                                               fill=k.fillreg(e, 0.0), base=0, channel_multiplier=-1), [("wsm",)], [("wsm",)])
        bsb = k.sb([128, 4, 128], F32, "bsb"); k.dma(bsb[:], sbs)
        uvt = k.sb([128, 8, Tq], F32, "uvt")
        vsq = k.sb([128, 4, Tq], F32, "vsq")
        vn = k.sb([128, 4, Tq], F32, "vn")
        mean = k.sb([128, Tq], F32, "mean"); msq = k.sb([128, Tq], F32, "msq"); srs = k.sb([128, Tq], F32, "srs")
        vtok = [k.sb([128, nq, 128], F32, "vtok%d" % i) for i in range(2)]
        yco = [k.sb([128, Tq], F32, "yco%d" % i) for i in range(2)]
        for t in range(L // Tq):
            sl = slice(t * Tq, (t + 1) * Tq)
            k.dma(uvt[:], uvsrc(sl))
            for cc in range(8):
                k.act(uvt[:, cc, :], uvt[:, cc, :], AF.Gelu_apprx_tanh, rk=[("uvt",)], wk=[("uvt", cc)])
            for cc in range(4):
                k.act(vsq[:, cc, :], uvt[:, 4 + cc, :], AF.Square, rk=[("uvt", 4 + cc)], wk=[("vsq", cc)])
            pm = nps(); pq = nps()
            for cc in range(4):
                k.mm(pm[:, :Tq], ones32[:], uvt[:, 4 + cc, :], start=(cc == 0), stop=(cc == 3))
            for cc in range(4):
                k.mm(pq[:, :Tq], ones32[:], vsq[:, cc, :], start=(cc == 0), stop=(cc == 3))
            k.act(mean[:], pm[:, :Tq], AF.Copy, scale=1.0 / 512)
            k.tt(msq[:], mean[:], mean[:], ALU.mult)
            k.stt(srs[:], pq[:, :Tq], 1.0 / 512, msq[:], ALU.mult, ALU.subtract)
            k.act(srs[:], srs[:], AF.Sqrt, bias=EPS)
            k.op("dve", lambda e: e.reciprocal(out=srs[:], in_=srs[:]), [("srs",)], [("srs",)])
            for cc in range(4):
                k.tt(vn[:, cc, :], uvt[:, 4 + cc, :], mean[:], ALU.subtract, wk=[("vn", cc)])
                k.tt(vn[:, cc, :], vn[:, cc, :], srs[:], ALU.mult, rk=[("vn", cc), ("srs",)], wk=[("vn", cc)])
                k.ts(vn[:, cc, :], vn[:, cc, :], sp_[:, cc:cc + 1], ALU.mult, sp_[:, 4 + cc:5 + cc], ALU.add,
                     rk=[("vn", cc)], wk=[("vn", cc)])
            for g in range(4):
                px = nps()
                for n in range(nq):
                    k.mm(px[:, n * 128:(n + 1) * 128], vn[:, g, n * 128:(n + 1) * 128], ident32[:], rk=[("vn", g), ("ident32",)])
                vt = vtok[g % 2]
                k.act(vt[:].rearrange("p n d -> p (n d)"), px[:, :Tq], AF.Copy)
                py = nps()
                for n in range(nq):
                    k.mm(py[:, n * 128:(n + 1) * 128], vt[:, n, :], wsm[:, g, :])
                yo_ = yco[g % 2]
                k.tt(yo_[:].rearrange("p (n t) -> p n t", n=nq), py[:, :Tq].rearrange("p (n t) -> p n t", n=nq),
                     bsb[:, g, :].unsqueeze(1).broadcast_to([128, nq, 128]), ALU.add)
                k.tt(yo_[:], yo_[:], uvt[:, g, :], ALU.mult)
                k.dma(Yf(8 + g)[:, sl], yo_[:])


    def head(j):
        if "c" in B_PARTS:
            Tb = min(512, L)
            scp = k.sb([128, 4], F32, "scp"); k.dma(scp[:], scw[j])
            mh = k.sb([128, Tb + 2], F32, "mh"); k.memset(mh[:, 0:2], 0.0)
            sA = [k.sb([128, Tb], F32, "sA%d" % i) for i in range(2)]
            sBt = [k.sb([128, Tb], F32, "sB%d" % i) for i in range(2)]
            sC = [k.sb([128, Tb], F32, "sC%d" % i) for i in range(2)]
            so = [k.sb([128, Tb], F32, "so%d" % i) for i in range(2)]
            for t in range(L // Tb):
                a_, b_, c_, o_ = sA[t % 2], sBt[t % 2], sC[t % 2], so[t % 2]
                sl = slice(t * Tb, (t + 1) * Tb)
                k.dma(a_[:], PTf(36 + j)[:, sl]); k.dma(b_[:], PTf(40 + j)[:, sl]); k.dma(c_[:], PTf(32 + j)[:, sl])
                k.tt(mh[:, 2:], a_[:], b_[:], ALU.mult)
                k.ts(o_[:], mh[:, 2:], scp[:, 2:3], ALU.mult)
                k.stt(o_[:], mh[:, 1:Tb + 1], scp[:, 1:2], o_[:], ALU.mult, ALU.add)
                k.stt(o_[:], mh[:, 0:Tb], scp[:, 0:1], o_[:], ALU.mult, ALU.add)
                k.tt(o_[:], o_[:], c_[:], ALU.mult)
                k.dma(Yf(12 + j)[:, sl], o_[:])
                k.copy(a_[:, 0:2], mh[:, Tb:Tb + 2], eng="pool")
                k.copy(mh[:, 0:2], a_[:, 0:2], eng="pool")

        if "l" in B_PARTS:
            Tb = min(512, L)
            lp = k.sb([128, 8], F32, "lp"); k.dma(lp[:], lpar[j])
            lwa = k.sb([128, 128], F32, "lwa"); k.dma(lwa[:], lw[j, 0])
            lwx = k.sb([128, 128], F32, "lwx"); k.dma(lwx[:], lw[j, 1])
            lc = k.sb([128, 4], F32, "lc")
            k.act(lc[:, 0:1], lp[:, 7:8], AF.Exp, scale=-1.0)
            k.act(lc[:, 0:1], lc[:, 0:1], AF.Ln, bias=1.0)
            k.ts(lc[:, 1:2], lc[:, 0:1], -8.0, ALU.mult)
            k.ts(lc[:, 2:3], lc[:, 0:1], -16.0, ALU.mult)
            xh = k.sb([128, Tb + 3], F32, "xh"); k.memset(xh[:, 0:3], 0.0)
            hprev = k.sb([128, 1], F32, "hprev"); k.memset(hprev[:], 0.0)
            tmp3 = k.sb([128, 4], F32, "tmp3")
            lG = [k.sb([128, Tb], F32, "lG%d" % i) for i in range(2)]
            xc = k.sb([128, Tb], F32, "xc")
            rr = k.sb([128, Tb], F32, "rr"); ii = k.sb([128, Tb], F32, "ii")
            aa = k.sb([128, Tb], F32, "aa"); a2 = k.sb([128, Tb], F32, "a2")
            hh = [k.sb([128, Tb], F32, "hh%d" % i) for i in range(2)]
            for t in range(L // Tb):
                sl = slice(t * Tb, (t + 1) * Tb)
                G = lG[t % 2]; h_ = hh[t % 2]
                k.dma(xh[:, 3:], PTf(16 + j)[:, sl]); k.dma(G[:], PTf(20 + j)[:, sl])
                k.ts(xc[:], xh[:, 3:], lp[:, 3:4], ALU.mult, lp[:, 4:5], ALU.add)
                for tap in range(3):
                    k.stt(xc[:], xh[:, tap:tap + Tb], lp[:, tap:tap + 1], xc[:], ALU.mult, ALU.add)
                k.copy(tmp3[:, 0:3], xh[:, Tb:Tb + 3], eng="pool")
                k.copy(xh[:, 0:3], tmp3[:, 0:3], eng="pool")
                for hf in range(Tb // 512 if Tb >= 512 else 1):
                    W = min(512, Tb)
                    cs = slice(hf * W, (hf + 1) * W)
                    p1 = nps(); k.mm(p1[:, :W], lwa[:], xc[:, cs])
                    k.act(rr[:, cs], p1[:, :W], AF.Sigmoid, bias=lp[:, 5:6])
                    p2 = nps(); k.mm(p2[:, :W], lwx[:], xc[:, cs])
                    k.act(ii[:, cs], p2[:, :W], AF.Sigmoid, bias=lp[:, 6:7])
                k.act(aa[:], rr[:], AF.Exp, scale=lc[:, 1:2])
                k.act(a2[:], rr[:], AF.Exp, scale=lc[:, 2:3])
                k.ts(a2[:], a2[:], -1.0, ALU.mult, 1.0, ALU.add)
                k.act(a2[:], a2[:], AF.Sqrt)
                k.tt(ii[:], ii[:], xc[:], ALU.mult)
                k.tt(ii[:], ii[:], a2[:], ALU.mult)
                k.op("dve", lambda e, h_=h_: e.tensor_tensor_scan(out=h_[:], data0=aa[:], data1=ii[:], initial=hprev[:, 0:1],
                                                                  op0=ALU.mult, op1=ALU.add),
                     [("aa",), ("ii",), ("hprev",)], [_key(h_[:])])
                k.copy(hprev[:], h_[:, Tb - 1:Tb], eng="pool")
                k.act(G[:], G[:], AF.Gelu_apprx_tanh)
                k.tt(h_[:], h_[:], G[:], ALU.mult)
                k.dma(Yf(4 + j)[:, sl], h_[:])

        if "g" in B_PARTS:
            gp = k.sb([128, 16], F32, "gp"); k.dma(gp[:], gpar[j])
            braw = k.sb([128, 64], F32, "braw"); araw = k.sb([128, 64], F32, "araw")
            k.dma(braw[0:nC, :], PBAf(j).rearrange("(n c) -> n c", c=64))
            k.dma(araw[0:nC, :], PBAf(4 + j).rearrange("(n c) -> n c", c=64))
            beta = k.sb([128, 64], F32, "beta"); gx = k.sb([128, 64], F32, "gx"); gt1 = k.sb([128, 64], F32, "gt1")
            gcum = k.sb([128, 64], F32, "gcum"); s1 = k.sb([128, 64], F32, "s1"); s3 = k.sb([128, 64], F32, "s3")
            ones64 = k.sb([128, 64], F32, "ones64"); k.memset(ones64[:], 1.0)
            nexpA = k.sb([128, 1], F32, "nexpA")
            k.act(nexpA[:], gp[:, 13:14], AF.Exp)
            k.ts(nexpA[:], nexpA[:], -1.0, ALU.mult)
            k.act(beta[0:nC, :], braw[0:nC, :], AF.Sigmoid)
            k.ts(gx[0:nC, :], araw[0:nC, :], gp[0:nC, 14:15], ALU.add)
            k.act(gt1[0:nC, :], gx[0:nC, :], AF.Abs)
            k.act(gt1[0:nC, :], gt1[0:nC, :], AF.Exp, scale=-1.0)
            k.act(gt1[0:nC, :], gt1[0:nC, :], AF.Ln, bias=1.0)
            k.ts(gx[0:nC, :], gx[0:nC, :], 0.0, ALU.max)
            k.tt(gx[0:nC, :], gx[0:nC, :], gt1[0:nC, :], ALU.add)
            k.ts(gx[0:nC, :], gx[0:nC, :], nexpA[0:nC, 0:1], ALU.mult)
            k.op("dve", lambda e: e.tensor_tensor_scan(out=gcum[0:nC, :], data0=ones64[0:nC, :], data1=gx[0:nC, :], initial=0.0,
                                                       op0=ALU.mult, op1=ALU.add), [("ones64",), ("gx",)], [("gcum",)])
            k.act(s1[0:nC, :], gcum[0:nC, :], AF.Exp)
            k.tt(s1[0:nC, :], s1[0:nC, :], beta[0:nC, :], ALU.mult)
            k.act(s3[0:nC, :], gcum[0:nC, :], AF.Exp, scale=-1.0, bias=gcum[0:nC, 63:64])
            colT = k.sb([64, 4, 128], F32, "colT")
            pc = nps()
            for wi, src in enumerate((gcum, beta, s1, s3)):
                k.mm(pc[0:64, wi * 128:wi * 128 + nC], src[0:nC, :], ident32[0:nC, 0:nC])
            for wi in range(4):
                k.act(colT[:, wi, 0:nC], pc[0:64, wi * 128:wi * 128 + nC], AF.Copy)

            GT = 512
            ng = L // GT
            qh = k.sb([128, 3, GT + 3], F32, "qh"); k.memset(qh[:, :, 0:3], 0.0)
            tmpq = k.sb([128, 3, 4], F32, "tmpq")
            zt = [k.sb([128, GT], F32, "zt%d" % i) for i in range(2)]
            cs_ = [k.sb([128, GT], F32, "cs%d" % i) for i in range(3)]
            Qb = k.sb([128, GT], BF16, "Qb"); Kb = k.sb([128, GT], BF16, "Kb"); Vb = k.sb([128, GT], BF16, "Vb")
            KBb = k.sb([128, GT], BF16, "KBb"); qg = k.sb([128, GT], BF16, "qg")
            rq = k.sb([128, GT], F32, "rq")
            rhsG = k.sb([64, 8, 64], F32, "rhsG"); rhsB = k.sb([64, 8, 64], F32, "rhsB")
            egbc = k.sb([128, GT], F32, "egbc")
            argU = k.sb([64, 8, 64], F32, "argU"); argL = k.sb([64, 8, 64], F32, "argL")
            decU = k.sb([64, 8, 64], F32, "decU"); decUs = k.sb([64, 8, 64], F32, "decUs"); decLs = k.sb([64, 8, 64], F32, "decLs")
            attnT = k.sb([64, GT], BF16, "attnT")
            Pb = [k.sb([64, GT], BF16, "Pb%d" % i) for i in range(2)]
            PTb = [k.sb([64, GT], BF16, "PTb%d" % i) for i in range(2)]
            TT32 = k.sb([64, GT], F32, "TT32"); TTb = k.sb([64, GT], BF16, "TTb")
            kbg = k.sb([64, 8, 128], BF16, "kbg"); kg = k.sb([64, 8, 128], BF16, "kg"); vbt = k.sb([64, 8, 128], BF16, "vbt")
            wTb = k.sb([128, GT], BF16, "wTb"); u32 = k.sb([64, 8, 128], F32, "u32")
            S32 = k.sb([128, 128], F32, "S32"); Sb = k.sb([128, 128], BF16, "Sb")
            k.memset(S32[:], 0.0); k.memset(Sb[:], 0.0)
            vnew = [k.sb([64, 128], BF16, "vnew%d" % i) for i in range(2)]
            o32 = k.sb([128, GT], F32, "o32")
            yat = [k.sb([128, GT], F32, "yat%d" % i) for i in range(2)]
            r3 = lambda ap: ap.rearrange("p (c j) -> p c j", c=8)

            for gi in range(ng):
                sl = slice(gi * GT, (gi + 1) * GT)
                c0 = gi * 8
                z_ = zt[gi % 2]
                for wq in range(3):
                    k.dma(qh[:, wq, 3:], PTf(4 * wq + j)[:, sl], wk=[("qh", wq)])
                k.dma(z_[:], PTf(12 + j)[:, sl])
                for wq in range(3):
                    o_ = cs_[wq]
                    k.ts(o_[:], qh[:, wq, 3:], gp[:, 4 * wq + 3:4 * wq + 4], ALU.mult, rk=[("qh", wq)])
                    for tap in range(3):
                        k.stt(o_[:], qh[:, wq, tap:tap + GT], gp[:, 4 * wq + tap:4 * wq + tap + 1], o_[:], ALU.mult, ALU.add,
                              rk=[("qh", wq), _key(o_[:])])
                    k.copy(tmpq[:, wq, 0:3], qh[:, wq, GT:GT + 3], eng="pool", rk=[("qh", wq)], wk=[("tmpq", wq)])
                    k.copy(qh[:, wq, 0:3], tmpq[:, wq, 0:3], eng="pool", rk=[("tmpq", wq)], wk=[("qh", wq)])
                    k.act(o_[:], o_[:], AF.Silu)
                r = c.rstd_of([cs_[0][:]], GT, 1.0, out=rq)
                k.stt(cs_[0][:], cs_[0][:], 128.0 ** -0.5, r[:], ALU.mult, ALU.mult)
                k.copy(Qb[:], cs_[0][:], eng="pool")
                r = c.rstd_of([cs_[1][:]], GT, 1.0, out=rq)
                k.tt(cs_[1][:], cs_[1][:], r[:], ALU.mult)
                k.copy(Kb[:], cs_[1][:], eng="pool")
                k.copy(Vb[:], cs_[2][:], eng="pool")
                GTrep = colT[:, 0, c0:c0 + 8].unsqueeze(2).broadcast_to([64, 8, 64])
                BTrep = colT[:, 1, c0:c0 + 8].unsqueeze(2).broadcast_to([64, 8, 64])
                k.tt(rhsG[:], identrep[:], GTrep, ALU.mult)
                k.tt(rhsB[:], identrep[:], BTrep, ALU.mult)
                pG = nps(); k.mm(pG[:, :], ones32[0:64, :], rhsG[:].rearrange("p c j -> p (c j)"))
                pB = nps(); k.mm(pB[:, :], ones32[0:64, :], rhsB[:].rearrange("p c j -> p (c j)"))
                k.act(egbc[:], pG[:, :], AF.Exp)
                k.tt(argU[:], r3(pG[0:64, :]), GTrep, ALU.subtract)
                k.op("pool", lambda e: e.affine_select(out=argU[:], in_=argU[:], pattern=[[0, 8], [1, 64]], compare_op=ALU.is_ge,
                                                       fill=k.fillreg(e, -30000.0), base=0, channel_multiplier=-1), [("argU",)], [("argU",)])
                k.act(decU[:], argU[:], AF.Exp)
                k.op("pool", lambda e: e.affine_select(out=decUs[:], in_=decU[:], pattern=[[0, 8], [1, 64]], compare_op=ALU.is_gt,
                                                       fill=k.fillreg(e, 0.0), base=0, channel_multiplier=-1), [("decU",)], [("decUs",)])
                k.tt(argL[:], GTrep, r3(pG[0:64, :]), ALU.subtract, rk=[("colT",), _key(pG[:])])
                k.op("pool", lambda e: e.affine_select(out=argL[:], in_=argL[:], pattern=[[0, 8], [-1, 64]], compare_op=ALU.is_gt,
                                                       fill=k.fillreg(e, -30000.0), base=0, channel_multiplier=1), [("argL",)], [("argL",)])
                k.act(decLs[:], argL[:], AF.Exp)
                k.tt(qg[:], cs_[0][:], egbc[:], ALU.mult)
                k.tt(KBb[:], cs_[1][:], pB[:, :], ALU.mult)
                pA = nps(); pMT = nps(); pM = nps()
                for ci in range(8):
                    cl = slice(ci * 64, ci * 64 + 64)
                    k.mm(pA[0:64, cl], Kb[:, cl], Qb[:, cl])
                    k.mm(pMT[0:64, cl], Kb[:, cl], KBb[:, cl])
                    k.mm(pM[0:64, cl], KBb[:, cl], Kb[:, cl])
                k.tt(attnT[:], pA[0:64, :], decU[:].rearrange("p c j -> p (c j)"), ALU.mult)
                k.stt(PTb[0][:], pMT[0:64, :], -1.0, decUs[:].rearrange("p c j -> p (c j)"), ALU.mult, ALU.mult)
                k.stt(Pb[0][:], pM[0:64, :], -1.0, decLs[:].rearrange("p c j -> p (c j)"), ALU.mult, ALU.mult)
                k.stt(TT32[:], pMT[0:64, :], -1.0, decUs[:].rearrange("p c j -> p (c j)"), ALU.mult, ALU.mult)
                k.tt(TT32[:], TT32[:], identrep[:].rearrange("p c j -> p (c j)"), ALU.add)
                k.copy(TTb[:], TT32[:], eng="pool")
                cur = 0
                for lv in range(1, 6):
                    nxt = cur ^ 1
                    pP = nps()
                    for ci in range(8):
                        cl = slice(ci * 64, ci * 64 + 64)
                        k.mm(pP[0:64, cl], PTb[cur][:, cl], Pb[cur][:, cl])
                    k.act(Pb[nxt][:], pP[0:64, :], AF.Copy)
                    if lv < 5:
                        pPT = nps()
                        for ci in range(8):
                            cl = slice(ci * 64, ci * 64 + 64)
                            k.mm(pPT[0:64, cl], Pb[cur][:, cl], PTb[cur][:, cl])
                        k.act(PTb[nxt][:], pPT[0:64, :], AF.Copy)
                    pT_ = nps()
                    for ci in range(8):
                        cl = slice(ci * 64, ci * 64 + 64)
                        k.mm(pT_[0:64, cl], Pb[nxt][:, cl], TTb[:, cl])
                    k.tt(TT32[:], TT32[:], pT_[0:64, :], ALU.add)
                    k.copy(TTb[:], TT32[:], eng="pool")
                    cur = nxt
                for hf in range(2):
                    pK = nps(); pV = nps()
                    for cj in range(4):
                        ci = hf * 4 + cj
                        cl = slice(ci * 64, ci * 64 + 64)
                        k.mm(pK[0:64, cj * 128:(cj + 1) * 128], Kb[:, cl], identb[:])
                        k.mm(pV[0:64, cj * 128:(cj + 1) * 128], Vb[:, cl], identb[:])
                    cc0 = c0 + hf * 4
                    S1rep = colT[:, 2, cc0:cc0 + 4].unsqueeze(2).broadcast_to([64, 4, 128])
                    S3rep = colT[:, 3, cc0:cc0 + 4].unsqueeze(2).broadcast_to([64, 4, 128])
                    Brep = colT[:, 1, cc0:cc0 + 4].unsqueeze(2).broadcast_to([64, 4, 128])
                    pK3 = pK[0:64, :].rearrange("p (c d) -> p c d", c=4)
                    pV3 = pV[0:64, :].rearrange("p (c d) -> p c d", c=4)
                    k.tt(kbg[:, hf * 4:hf * 4 + 4, :], pK3, S1rep, ALU.mult, wk=[("kbg", hf)])
                    k.tt(kg[:, hf * 4:hf * 4 + 4, :], pK3, S3rep, ALU.mult, wk=[("kg", hf)])
                    k.tt(vbt[:, hf * 4:hf * 4 + 4, :], pV3, Brep, ALU.mult, wk=[("vbt", hf)])
                pW = nps()
                for ci in range(8):
                    cl = slice(ci * 64, ci * 64 + 64)
                    k.mm(pW[:, cl], kbg[:, ci, :], TTb[:, cl])
                k.act(wTb[:], pW[:, :], AF.Copy)
                for hf in range(2):
                    pU = nps()
                    for cj in range(4):
                        ci = hf * 4 + cj
                        cl = slice(ci * 64, ci * 64 + 64)
                        k.mm(pU[0:64, cj * 128:(cj + 1) * 128], TTb[:, cl], vbt[:, ci, :])
                    k.act(u32[:, hf * 4:hf * 4 + 4, :].rearrange("p c d -> p (c d)"), pU[0:64, :], AF.Copy, wk=[("u32", hf)])
                for ci in range(8):
                    cl = slice(ci * 64, ci * 64 + 64)
                    vn_ = vnew[ci % 2]
                    pR = nps()
                    k.mm(pR[0:64, 0:128], wTb[:, cl], Sb[:])
                    k.tt(vn_[:], u32[:, ci, :], pR[0:64, 0:128], ALU.subtract)
                    k.mm(psO[:, cl], Sb[:], qg[:, cl], start=True, stop=False)
                    k.mm(psO[:, cl], vn_[:], attnT[:, cl], start=False, stop=True)
                    k.mm(pR[:, 128:256], kg[:, ci, :], vn_[:])
                    k.stt(S32[:], S32[:], egbc[:, ci * 64 + 63:ci * 64 + 64], pR[:, 128:256], ALU.mult, ALU.add)
                    k.act(Sb[:], S32[:], AF.Copy)
                k.act(o32[:], psO[:, :], AF.Copy)
                r = c.rstd_of([o32[:]], GT, 128.0, out=rq)
                y_ = yat[gi % 2]
                k.stt(y_[:], o32[:], gp[:, 12:13], r[:], ALU.mult, ALU.mult)
                k.act(z_[:], z_[:], AF.Silu)
                k.tt(y_[:], y_[:], z_[:], ALU.mult)
                k.dma(Yf(j)[:, sl], y_[:])


    k.prefix = base_prefix + "sg_"
    if "s" in B_PARTS:
        with k.scope():
            sgu_part()
        k.barrier()
    for j in range(4 if "h" in B_PARTS else 0):
        k.prefix = base_prefix + "h%d_" % j
        with k.scope():
            head(j)
        k.barrier()
    k.prefix = base_prefix


def build_fused(L, depth):
    nc = bass.Bass("TRN2", target_bir_lowering=False)
    c = Ctx(nc); k = c.k
    xT = c.dram("xT", [128, 16, L])
    xo = c.dram("xo", [128, 16, L], kind="ExternalOutput")
    PT = nc.dram_tensor("PT", [44, 128, L], F32)
    PBA = nc.dram_tensor("PBA", [8, L], F32)
    Y = nc.dram_tensor("Y", [16, 128, L], F32)
    PTa = PT.ap(); PBAa = PBA.ap(); Ya = Y.ap()
    uv_view = PTa[24:32].rearrange("c p t -> p c t")
    Y_view = Ya.rearrange("c p t -> p c t")
    for l in range(depth):
        sfx = "_%d" % l
        gam = c.dram("gam" + sfx, [128, 16]); w = c.dram("w" + sfx, [11, 128, 16, 512]); wba = c.dram("wba" + sfx, [128, 16, 8])
        gpar = c.dram("gpar" + sfx, [4, 128, 16]); lpar = c.dram("lpar" + sfx, [4, 128, 8]); lw = c.dram("lw" + sfx, [4, 2, 128, 128])
        spar = c.dram("spar" + sfx, [128, 8]); swT = c.dram("swT" + sfx, [128, 4, 128]); sbs = c.dram("sbs" + sfx, [128, 4, 128])
        scw = c.dram("scw" + sfx, [4, 128, 4])
        vecs = c.dram("vecs" + sfx, [128, 60]); wout = c.dram("wout" + sfx, [8, 128, 16, 256])
        wup = c.dram("wup" + sfx, [44, 128, 16, 256]); cw = c.dram("cw" + sfx, [128, 88, 4]); wdn = c.dram("wdn" + sfx, [16, 128, 44, 128])
        xin = xT if l == 0 else xo
        k.prefix = "L%dA_" % l
        with k.scope():
            emit_A(c, L, xin, gam, w, wba, lambda ch: PTa[ch], PBAa)
        k.barrier()
        k.prefix = "L%dB_" % l
        with k.scope():
            emit_B(c, L, lambda ch: PTa[ch], lambda r: PBAa[r], lambda sl: uv_view[:, :, sl], lambda ch: Ya[ch],
                   gpar, lpar, lw, spar, swT, sbs, scw)
        k.barrier()
        k.prefix = "L%dC_" % l
        with k.scope():
            emit_C(c, L, lambda c0, T: xin[:, :, c0:c0 + T], lambda g, c0, T: Y_view[:, 4 * g:4 * g + 4, c0:c0 + T],
                   lambda c0, T: xo[:, :, c0:c0 + T], vecs, wout, wup, cw, wdn)
        k.barrier()
    k.prefix = ""
    k.finish()
    k.build()
    return nc


_PROGS = {}
_PERM = np.concatenate([np.arange(0, 2048), np.arange(2056, 5640)])


def _fm(v):
    return np.ascontiguousarray(v.reshape(-1, 128).T)


def host_weights(P, l):
    d = {}
    sfx = "_%d" % l
    W = P["w_in"][l]
    d["w"] = np.ascontiguousarray(W[:, _PERM].reshape(16, 128, 11, 512).transpose(2, 1, 0, 3))
    d["wba"] = np.ascontiguousarray(W[:, 2048:2056].reshape(16, 128, 8).transpose(1, 0, 2))
    d["gam"] = _fm(P["pre_mix_norm"][l])
    gpar = np.zeros((4, 128, 16), np.float32)
    lpar = np.zeros((4, 128, 8), np.float32)
    lw = np.zeros((4, 2, 128, 128), np.float32)
    scw = np.zeros((4, 128, 4), np.float32)
    for j in range(4):
        cs = slice(j * 128, (j + 1) * 128)
        for wq in range(3):
            gpar[j, :, 4 * wq:4 * wq + 4] = P["gdn_conv_w"][l][:, wq * 512 + j * 128: wq * 512 + (j + 1) * 128].T
        gpar[j, :, 12] = P["gdn_norm_w"][l]
        gpar[j, :, 13] = P["gdn_a_log"][l][j]
        gpar[j, :, 14] = P["gdn_dt_bias"][l][j]
        lpar[j, :, 0:4] = P["lru_conv_w"][l][:, cs].T
        lpar[j, :, 4] = P["lru_conv_b"][l][cs]
        lpar[j, :, 5] = P["lru_ba"][l].reshape(-1)[cs]
        lpar[j, :, 6] = P["lru_bx"][l].reshape(-1)[cs]
        lpar[j, :, 7] = P["lru_lambda"][l][cs]
        for q, nm in enumerate(("lru_wa", "lru_wx")):
            lw[j, q, 0:64, 0:64] = P[nm][l][2 * j]
            lw[j, q, 64:128, 64:128] = P[nm][l][2 * j + 1]
        scw[j, :, 0:3] = P["sconv_w"][l][:, cs].T
    d["gpar"] = gpar; d["lpar"] = lpar; d["lw"] = lw; d["scw"] = scw
    d["spar"] = np.concatenate([_fm(P["sgu_ln_w"][l]), _fm(P["sgu_ln_b"][l])], axis=1)
    d["swT"] = np.ascontiguousarray(P["sgu_ws"][l].transpose(2, 0, 1))
    d["sbs"] = np.ascontiguousarray(np.broadcast_to(P["sgu_b"][l][None], (128, 4, 128)))
    vecs = np.zeros((128, 60), np.float32)
    for g in range(3):
        vecs[:, 4 * g:4 * g + 4] = _fm(P["grp_norm_w"][l][g])
    vecs[:, 12:28] = _fm(P["post_mix_norm"][l])
    vecs[:, 28:44] = _fm(P["pre_ffn_norm"][l])
    vecs[:, 44:60] = _fm(P["post_ffn_norm"][l])
    d["vecs"] = vecs
    d["wout"] = np.ascontiguousarray(P["w_out"][l].reshape(16, 128, 8, 256).transpose(2, 1, 0, 3))
    up = P["ffn_up"][l]
    wg = up[:, :5632].reshape(16, 128, 44, 128)
    wv = up[:, 5632:].reshape(16, 128, 44, 128)
    d["wup"] = np.ascontiguousarray(np.stack([wg, wv], axis=3).transpose(2, 1, 0, 3, 4).reshape(44, 128, 16, 256))
    cw = np.zeros((128, 88, 4), np.float32)
    cwf = P["ffn_conv_w"][l]
    cbf = P["ffn_conv_b"][l]
    for gv in range(2):
        blk = cwf[:, gv * 5632:(gv + 1) * 5632].reshape(3, 44, 128)
        cw[:, gv::2, 0:3] = blk.transpose(2, 1, 0)
        cw[:, gv::2, 3] = cbf[gv * 5632:(gv + 1) * 5632].reshape(44, 128).T
    d["cw"] = cw
    d["wdn"] = np.ascontiguousarray(P["ffn_down"][l].reshape(44, 128, 16, 128).transpose(2, 1, 0, 3))
    return {kk + sfx: v for kk, v in d.items()}


def kernel(**inputs):
    x = np.asarray(inputs["x"], np.float32)
    P = {kk: np.asarray(v, np.float32) for kk, v in inputs.items() if kk != "x"}
    B, S, _ = x.shape
    depth = P["w_in"].shape[0]
    key = (S, depth)
    if key not in _PROGS:
        _PROGS[key] = build_fused(S, depth)
    nc = _PROGS[key]
    wts = {}
    for l in range(depth):
        wts.update(host_weights(P, l))
    maps = []
    for b in range(B):
        m = dict(wts)
        m["xT"] = np.ascontiguousarray(x[b].reshape(S, 16, 128).transpose(2, 1, 0))
        maps.append(m)
    res = run_bass_kernel_spmd(nc, maps, core_ids=list(range(B))).results
    out = np.empty((B, S, 2048), np.float32)
    for b in range(B):
        out[b] = res[b]["xo"].transpose(2, 1, 0).reshape(S, 2048)
    return out
```

```python
import numpy as np
from contextlib import ExitStack
import concourse.bass as bass
import concourse.mybir as mybir
from concourse.bass_utils import run_bass_kernel_spmd

F32 = mybir.dt.float32
BF16 = mybir.dt.bfloat16
AF = mybir.ActivationFunctionType
ALU = mybir.AluOpType

EPS = 1e-6
DM = 2048
NCORES = 8
SAME_ENGINE_SYNC = True
N_DMA_SLOTS = 16


_ALIAS = {}
B_PARTS = "shclg"
SCAN_W = 128


def _key(ap):
    n = ap.name
    return (_ALIAS.get(n, n),)


class KB:
    ENGS = ("pe", "act", "dve", "pool", "sp")

    def __init__(self, nc):
        self.nc = nc
        self.es = ExitStack()
        self.items = {e: [] for e in self.ENGS}
        self.seq = {e: 0 for e in self.ENGS}
        self.sem = {}
        for e in ("pe", "act", "dve", "pool"):
            self.sem[e] = self.es.enter_context(nc.semaphore("s_" + e))
        self.dsem = [self.es.enter_context(nc.semaphore("d%d" % i)) for i in range(N_DMA_SLOTS)]
        self.dcount = [0] * N_DMA_SLOTS
        self.dnext = 0
        self.dnextq = [0, 0]
        self.last_w = {}
        self.readers = {}
        self.subs = {}
        self.waited = {e: {} for e in self.ENGS}
        self.ntile = 0

    prefix = ""

    def sb(self, shape, dt=F32, name=None):
        self.ntile += 1
        base = name or ("t%d" % self.ntile)
        _ALIAS[self.prefix + base] = base
        return self.es.enter_context(self.nc.sbuf_tensor(self.prefix + base, list(shape), dt))

    def ps(self, shape=(128, 512), dt=F32, name=None):
        self.ntile += 1
        base = name or ("p%d" % self.ntile)
        _ALIAS[self.prefix + base] = base
        return self.es.enter_context(self.nc.psum_tensor(self.prefix + base, list(shape), dt))

    def _expand(self, key):
        name = key[0]
        if len(key) == 1:
            return [key] + [(name, s) for s in self.subs.get(name, ())]
        self.subs.setdefault(name, set()).add(key[1])
        return [key, (name,)]

    def _need(self, eng, tok, waits):
        if tok is None:
            return
        if tok[0] == "c" and tok[1] == eng:
            if eng in ("pe", "sp") or not SAME_ENGINE_SYNC:
                return
        key = tok[:2]
        if self.waited[eng].get(key, 0) >= tok[2]:
            return
        self.waited[eng][key] = tok[2]
        waits.append(tok)

    def _deps(self, eng, reads, writes):
        waits = []
        for r in reads:
            for kk in self._expand(r):
                self._need(eng, self.last_w.get(kk), waits)
        for w in writes:
            for kk in self._expand(w):
                self._need(eng, self.last_w.get(kk), waits)
                for tok in self.readers.get(kk, ()):
                    self._need(eng, tok, waits)
        return waits

    def _commit(self, tok, reads, writes):
        for r in reads:
            lst = self.readers.setdefault(r, [])
            lst[:] = [t for t in lst if t[:2] != tok[:2]]
            lst.append(tok)
        for w in writes:
            self.last_w[w] = tok
            self.readers[w] = []
            if len(w) == 1:
                for s in self.subs.get(w[0], ()):
                    self.last_w[(w[0], s)] = tok
                    self.readers[(w[0], s)] = []

    def op(self, eng, fn, reads=(), writes=()):
        reads = list(reads); writes = list(writes)
        waits = self._deps(eng, reads, writes)
        self.seq[eng] += 1
        tok = ("c", eng, self.seq[eng])
        self.items[eng].append((waits, fn, ("c", eng)))
        self._commit(tok, reads, writes)

    def dma(self, out, in_, eng="sp", rk=None, wk=None):
        reads = rk if rk is not None else ([_key(in_)] if in_.name in self.sbnames else [])
        writes = wk if wk is not None else ([_key(out)] if out.name in self.sbnames else [])
        half = N_DMA_SLOTS // 2
        qi = 0 if eng == "sp" else 1
        slot = qi * half + self.dnextq[qi]
        self.dnextq[qi] = (self.dnextq[qi] + 1) % half
        waits = self._deps(eng, reads, writes)
        if self.dcount[slot] > 0:
            self._need(eng, ("d", slot, self.dcount[slot]), waits)
        self.dcount[slot] += 16
        tok = ("d", slot, self.dcount[slot])
        self.items[eng].append((waits, (lambda e: e.dma_start(out=out, in_=in_)), ("d", slot)))
        self._commit(tok, reads, writes)

    def cc(self, fn, reads=(), writes=()):
        if not hasattr(self, "ccsem"):
            self.ccsem = self.es.enter_context(self.nc.semaphore("cc_sem_kb"))
            self.cccount = 0
        reads = list(reads); writes = list(writes)
        waits = self._deps("pool", reads, writes)
        if self.cccount > 0:
            self._need("pool", ("x", 0, self.cccount), waits)
        self.cccount += 1
        tok = ("x", 0, self.cccount)
        self.items["pool"].append((waits, fn, ("x", 0)))
        self._commit(tok, reads, writes)

    sbnames = None

    def act(self, out, in_, func, bias=None, scale=None, rk=None, wk=None):
        reads = list(rk) if rk is not None else [_key(in_)]
        kw = {}
        if bias is not None:
            kw["bias"] = bias
            if not isinstance(bias, (int, float)):
                reads.append(_key(bias))
        if scale is not None:
            kw["scale"] = scale
            if not isinstance(scale, (int, float)):
                reads.append(_key(scale))
        self.op("act", lambda e: e.activation(out=out, in_=in_, func=func, **kw), reads, wk if wk is not None else [_key(out)])

    def mm(self, out, lhsT, rhs, start=True, stop=True, rk=None, wk=None):
        self.op("pe", lambda e: e.matmul(out, lhsT=lhsT, rhs=rhs, start=start, stop=stop),
                rk if rk is not None else [_key(lhsT), _key(rhs)], wk if wk is not None else [_key(out)])

    def tt(self, out, in0, in1, op, eng="dve", rk=None, wk=None):
        self.op(eng, lambda e: e.tensor_tensor(out=out, in0=in0, in1=in1, op=op),
                rk if rk is not None else [_key(in0), _key(in1)], wk if wk is not None else [_key(out)])

    def ts(self, out, in0, s1, op0, s2=None, op1=None, eng="dve", rk=None, wk=None):
        reads = list(rk) if rk is not None else [_key(in0)]
        for s in (s1, s2):
            if s is not None and not isinstance(s, (int, float)):
                reads.append(_key(s))
        if op1 is None:
            fn = lambda e: e.tensor_scalar(out=out, in0=in0, scalar1=s1, scalar2=None, op0=op0)
        else:
            fn = lambda e: e.tensor_scalar(out=out, in0=in0, scalar1=s1, scalar2=s2, op0=op0, op1=op1)
        self.op(eng, fn, reads, wk if wk is not None else [_key(out)])

    def stt(self, out, in0, scalar, in1, op0, op1, rk=None, wk=None):
        reads = list(rk) if rk is not None else [_key(in0), _key(in1)]
        if not isinstance(scalar, (int, float)):
            reads.append(_key(scalar))
        self.op("dve", lambda e: e.scalar_tensor_tensor(out=out, in0=in0, scalar=scalar, in1=in1, op0=op0, op1=op1),
                reads, wk if wk is not None else [_key(out)])

    def copy(self, out, in_, eng="dve", rk=None, wk=None):
        self.op(eng, lambda e: e.tensor_copy(out=out, in_=in_), rk if rk is not None else [_key(in_)],
                wk if wk is not None else [_key(out)])

    def memset(self, ap, val, eng="pool"):
        self.op(eng, lambda e: e.memset(ap, val), [], [_key(ap)])

    def fillreg(self, e, val):
        if not hasattr(self, "_regs"):
            self._regs = {}
        if val not in self._regs:
            self._regs[val] = e.to_reg(val)
        return self._regs[val]

    def scope(self):
        kb = self

        class _S:
            def __enter__(s_):
                s_.outer = kb.es
                kb.es = ExitStack()
                return kb

            def __exit__(s_, *a):
                kb.es.close()
                kb.es = s_.outer
                return False
        return _S()

    def barrier(self):
        toks = [("c", e, self.seq[e]) for e in ("pe", "act", "dve", "pool") if self.seq[e]]
        toks += [("d", s_, self.dcount[s_]) for s_ in range(N_DMA_SLOTS // 2) if self.dcount[s_]]
        for eng in self.ENGS:
            waits = []
            for tok in toks:
                self._need(eng, tok, waits)
            if waits:
                self.items[eng].append((waits, None, None))

    def finish(self):
        waits = []
        for slot in range(N_DMA_SLOTS):
            if self.dcount[slot]:
                self._need("sp", ("d", slot, self.dcount[slot]), waits)
        for e in ("pe", "act", "dve", "pool"):
            if self.seq[e]:
                self._need("sp", ("c", e, self.seq[e]), waits)
        if getattr(self, "cccount", 0):
            self._need("sp", ("x", 0, self.cccount), waits)
        self.items["sp"].append((waits, None, None))

    def build(self):
        nc = self.nc
        block = self.es.enter_context(nc.Block())

        def run(engname):
            def f(eng):
                for waits, fn, kind in self.items[engname]:
                    for tok in waits:
                        if tok[0] == "c":
                            eng.wait_ge(self.sem[tok[1]], tok[2])
                        elif tok[0] == "x":
                            eng.wait_ge(self.ccsem, tok[2])
                        else:
                            eng.wait_ge(self.dsem[tok[1]], tok[2])
                    if fn is None:
                        continue
                    ins = fn(eng)
                    if kind[0] == "c":
                        ins.then_inc(self.sem[kind[1]], 1)
                    elif kind[0] == "x":
                        ins.then_inc(self.ccsem, 1)
                    else:
                        ins.then_inc(self.dsem[kind[1]], 16)
            return f

        block.tensor(run("pe"))
        block.scalar(run("act"))
        block.vector(run("dve"))
        block.gpsimd(run("pool"))
        block.sync(run("sp"))
        self.es.close()


class Ctx:
    def __init__(self, nc):
        self.nc = nc
        self.k = KB(nc)
        k = self.k
        k.sbnames = set()
        _sb = k.sb

        def sb(shape, dt=F32, name=None):
            t = _sb(shape, dt, name)
            k.sbnames.add(t[:].name)
            return t
        k.sb = sb
        _ps = k.ps

        def ps(shape=(128, 512), dt=F32, name=None):
            t = _ps(shape, dt, name)
            k.sbnames.add(t[:].name)
            return t
        k.ps = ps
        self.ones = k.sb([128, 128], BF16, "ones")
        k.memset(self.ones[:], 1.0)
        self.sq = [k.sb([128, 512], BF16, "sq%d" % i) for i in range(2)]
        self.sqi = 0
        self.rstd = k.sb([128, 512], F32, "rstd")
        self.ps_stat = k.ps(name="ps_stat")

    def dram(self, name, shape, dt=F32, kind="ExternalInput"):
        return self.nc.dram_tensor(name, list(shape), dt, kind=kind).ap()

    def rstd_of(self, chunks, T, D, out=None, ps=None):
        k = self.k
        out = out if out is not None else self.rstd
        ps = ps if ps is not None else self.ps_stat
        n = len(chunks)
        for i, c in enumerate(chunks):
            s = self.sq[self.sqi]; self.sqi ^= 1
            k.act(s[:, :T], c, AF.Square)
            k.mm(ps[:, :T], self.ones[:], s[:, :T], start=(i == 0), stop=(i == n - 1))
        k.act(out[:, :T], ps[:, :T], AF.Sqrt, bias=EPS, scale=1.0 / D)
        k.op("dve", lambda e: e.reciprocal(out=out[:, :T], in_=out[:, :T]), [_key(out[:])], [_key(out[:])])
        return out


def emit_A(c, n_tok, xT, gam, w, wba, pT, pba, w_eng="pool", wkey=None):
    k = c.k
    T = min(512, n_tok)
    SBT = min(2048, n_tok)
    nsb = n_tok // SBT; nt = SBT // T
    gam_sb = k.sb([128, 16], F32, "gam_sb")
    k.dma(gam_sb[:], gam)
    wba_sb = k.sb([128, 16, 8], BF16, "wba_sb")
    k.dma(wba_sb[:], wba, eng="pool")
    xt = [k.sb([128, 16, T], F32, "xt%d" % i) for i in range(2)]
    xn = k.sb([128, 16, SBT], BF16, "xn")
    wsl = [k.sb([128, 16, 512], BF16, "wsl%d" % i) for i in range(2)]
    osb = [k.sb([128, T], F32, "osb%d" % i) for i in range(4)]
    psb = [k.ps(name="psA%d" % i) for i in range(4)]
    cnt = 0; xi = 0; wi = 0
    for sb_ in range(nsb):
        base = sb_ * SBT
        for t in range(nt):
            x_ = xt[xi % 2]; xi += 1
            k.dma(x_[:], xT[:, :, base + t * T:base + (t + 1) * T])
            r = c.rstd_of([x_[:, kk, :] for kk in range(16)], T, DM)
            for kk in range(16):
                k.stt(xn[:, kk, t * T:(t + 1) * T], x_[:, kk, :], gam_sb[:, kk:kk + 1], r[:, :T], ALU.mult, ALU.mult,
                      wk=[("xn", t)])
        for s in range(11):
            ws = wsl[wi % 2]; wi += 1
            k.dma(ws[:], w[s], eng=w_eng, rk=([(wkey, s)] if wkey else None))
            for t in range(nt):
                for cc in range(4):
                    ch = s * 4 + cc
                    ps = psb[cnt % 4]; ob = osb[cnt % 4]
                    for kk in range(16):
                        k.mm(ps[:, :T], ws[:, kk, cc * 128:(cc + 1) * 128], xn[:, kk, t * T:(t + 1) * T],
                             start=(kk == 0), stop=(kk == 15), rk=[_key(ws[:]), ("xn", t)])
                    if cnt % 2 == 0:
                        k.act(ob[:], ps[:, :T], AF.Copy)
                    else:
                        k.copy(ob[:], ps[:, :T])
                    k.dma(pT(ch)[:, base + t * T:base + (t + 1) * T], ob[:])
                    cnt += 1
        for t in range(nt):
            ps = psb[cnt % 4]; ob = osb[cnt % 4]
            for kk in range(16):
                k.mm(ps[0:8, :T], wba_sb[:, kk, :], xn[:, kk, t * T:(t + 1) * T], start=(kk == 0), stop=(kk == 15),
                     rk=[_key(wba_sb[:]), ("xn", t)])
            k.act(ob[0:8, :], ps[0:8, :T], AF.Copy)
            k.dma(pba[:, base + t * T:base + (t + 1) * T], ob[0:8, :])
            cnt += 1


def emit_C(c, n_tok, xsrc, ysrc, xdst, vecs, wout, wup, cw, wdn, w_eng="pool", wkeys=None):
    k = c.k
    T = min(512, n_tok); nt = n_tok // T
    vec_sb = k.sb([128, 60], F32, "vec_sb"); k.dma(vec_sb[:], vecs)
    cw_sb = k.sb([128, 88, 4], F32, "cw_sb"); k.dma(cw_sb[:], cw)
    GNW, PMN, PFN, PFFN = 0, 12, 28, 44
    xt = k.sb([128, 16, T], F32, "xt")
    yo = k.sb([128, 16, T], F32, "yo")
    nb = k.sb([128, 16, T], BF16, "nb")
    actb = k.sb([128, 44, T], BF16, "actb")
    wsl = [k.sb([128, 16, 256], BF16, "wsl%d" % i) for i in range(2)]
    wdl = [k.sb([128, 44, 128], BF16, "wdl%d" % i) for i in range(2)]
    hb = [k.sb([128, T + 2], F32, "hb%d" % i) for i in range(4)]
    cv = [k.sb([128, T], F32, "cv%d" % i) for i in range(4)]
    gl = [k.sb([128, T], F32, "gl%d" % i) for i in range(2)]
    halo = k.sb([128, 88, 2], F32, "halo")
    k.memset(halo[:], 0.0)
    psb = [k.ps(name="psC%d" % i) for i in range(4)]
    st = {"ps": 0, "w": 0, "wd": 0, "hb": 0}

    def tile(col0, Tt, halo_only, ocol0):
        k.dma(xt[:, :, :Tt], xsrc(col0, Tt))
        for g in range(4):
            stg = yo[:, 8 + 4 * (g % 2):12 + 4 * (g % 2), :Tt]
            skeys = [("yo", 8 + 4 * (g % 2) + i) for i in range(4)]
            k.dma(stg, ysrc(g, col0, Tt), wk=skeys)
            if g == 0:
                for cc in range(4):
                    k.act(nb[:, cc, :Tt], stg[:, cc, :], AF.Copy, rk=[skeys[cc]], wk=[("nb", cc)])
            else:
                for i in range(4):
                    s = c.sq[c.sqi]; c.sqi ^= 1
                    k.act(s[:, :Tt], stg[:, i, :], AF.Square, rk=[skeys[i]])
                    k.mm(c.ps_stat[:, :Tt], c.ones[:], s[:, :Tt], start=(i == 0), stop=(i == 3))
                k.act(c.rstd[:, :Tt], c.ps_stat[:, :Tt], AF.Sqrt, bias=EPS, scale=1.0 / 512)
                k.op("dve", lambda e: e.reciprocal(out=c.rstd[:, :Tt], in_=c.rstd[:, :Tt]), [("rstd",)], [("rstd",)])
                for cc in range(4):
                    col = GNW + (g - 1) * 4 + cc
                    k.stt(nb[:, 4 * g + cc, :Tt], stg[:, cc, :], vec_sb[:, col:col + 1], c.rstd[:, :Tt], ALU.mult, ALU.mult,
                          rk=[skeys[cc], ("rstd",)], wk=[("nb", 4 * g + cc)])
        for s in range(8):
            ws = wsl[st["w"] % 2]; st["w"] += 1
            k.dma(ws[:], wout[s], eng=w_eng, rk=([(wkeys[0], s)] if wkeys else None))
            for cc in range(2):
                ch = 2 * s + cc
                ps = psb[st["ps"] % 4]; st["ps"] += 1
                for kk in range(16):
                    k.mm(ps[:, :Tt], ws[:, kk, cc * 128:(cc + 1) * 128], nb[:, kk, :Tt], start=(kk == 0), stop=(kk == 15))
                k.act(yo[:, ch, :Tt], ps[:, :Tt], AF.Copy, wk=[("yo", ch)])
        r = c.rstd_of([yo[:, kk, :Tt] for kk in range(16)], Tt, DM)
        for kk in range(16):
            k.stt(yo[:, kk, :Tt], yo[:, kk, :Tt], vec_sb[:, PMN + kk:PMN + kk + 1], r[:, :Tt], ALU.mult, ALU.mult,
                  rk=[("yo", kk), ("rstd",)], wk=[("yo", kk)])
            k.tt(xt[:, kk, :Tt], xt[:, kk, :Tt], yo[:, kk, :Tt], ALU.add, rk=[("xt", kk), ("yo", kk)], wk=[("xt", kk)])
        r = c.rstd_of([xt[:, kk, :Tt] for kk in range(16)], Tt, DM)
        for kk in range(16):
            k.stt(nb[:, kk, :Tt], xt[:, kk, :Tt], vec_sb[:, PFN + kk:PFN + kk + 1], r[:, :Tt], ALU.mult, ALU.mult,
                  rk=[("xt", kk), ("rstd",)], wk=[("nb", kk)])
        for j in range(44):
            ws = wsl[st["w"] % 2]; st["w"] += 1
            k.dma(ws[:], wup[j], eng=w_eng, rk=([(wkeys[1], j)] if wkeys else None))
            cvs = []
            for gv in range(2):
                idx = 2 * j + gv
                ps = psb[st["ps"] % 4]; st["ps"] += 1
                hbt = hb[st["hb"] % 4]; cvt = cv[st["hb"] % 4]; st["hb"] += 1
                for kk in range(16):
                    k.mm(ps[:, :Tt], ws[:, kk, gv * 128:(gv + 1) * 128], nb[:, kk, :Tt], start=(kk == 0), stop=(kk == 15))
                k.copy(hbt[:, 0:2], halo[:, idx, :], eng="pool", rk=[("halo", idx)])
                k.act(hbt[:, 2:2 + Tt], ps[:, :Tt], AF.Copy)
                k.copy(halo[:, idx, :], hbt[:, Tt:Tt + 2], eng="pool", wk=[("halo", idx)])
                if halo_only:
                    continue
                k.ts(cvt[:, :Tt], hbt[:, 2:2 + Tt], cw_sb[:, idx, 2:3], ALU.mult, cw_sb[:, idx, 3:4], ALU.add)
                k.stt(cvt[:, :Tt], hbt[:, 1:1 + Tt], cw_sb[:, idx, 1:2], cvt[:, :Tt], ALU.mult, ALU.add)
                k.stt(cvt[:, :Tt], hbt[:, 0:Tt], cw_sb[:, idx, 0:1], cvt[:, :Tt], ALU.mult, ALU.add)
                cvs.append(cvt)
            if halo_only:
                continue
            g_ = gl[j % 2]
            k.act(g_[:, :Tt], cvs[0][:, :Tt], AF.Gelu_apprx_tanh)
            k.tt(actb[:, j, :Tt], g_[:, :Tt], cvs[1][:, :Tt], ALU.mult, wk=[("actb", j)])
        if halo_only:
            return
        for ch in range(16):
            wd = wdl[st["wd"] % 2]; st["wd"] += 1
            k.dma(wd[:], wdn[ch], eng=w_eng, rk=([(wkeys[2], ch)] if wkeys else None))
            ps = psb[st["ps"] % 4]; st["ps"] += 1
            for kk in range(44):
                k.mm(ps[:, :Tt], wd[:, kk, :], actb[:, kk, :Tt], start=(kk == 0), stop=(kk == 43))
            k.act(yo[:, ch, :Tt], ps[:, :Tt], AF.Copy, wk=[("yo", ch)])
        r = c.rstd_of([yo[:, kk, :Tt] for kk in range(16)], Tt, DM)
        for kk in range(16):
            k.stt(yo[:, kk, :Tt], yo[:, kk, :Tt], vec_sb[:, PFFN + kk:PFFN + kk + 1], r[:, :Tt], ALU.mult, ALU.mult,
                  rk=[("yo", kk), ("rstd",)], wk=[("yo", kk)])
            k.tt(xt[:, kk, :Tt], xt[:, kk, :Tt], yo[:, kk, :Tt], ALU.add, rk=[("xt", kk), ("yo", kk)], wk=[("xt", kk)])
        k.dma(xdst(ocol0, Tt), xt[:, :, :Tt])

    for t in range(nt):
        tile(t * T, T, False, t * T)


def emit_B(c, L, PTf, PBAf, uvsrc, Yf, gpar, lpar, lw, spar, swT, sbs, scw):
    k = c.k
    nC = L // 64
    assert nC <= 128
    base_prefix = k.prefix
    NB = 6
    psb = [k.ps(name="psB%d" % i) for i in range(NB)]
    psO = k.ps(name="psO")
    st = {"ps": 0}

    def nps():
        p = psb[st["ps"] % NB]; st["ps"] += 1
        return p

    ones32 = k.sb([128, 128], F32, "ones32"); k.memset(ones32[:], 1.0)
    ident32 = k.sb([128, 128], F32, "ident32")
    k.op("pool", lambda e: e.affine_select(out=ident32[:], in_=ones32[:], pattern=[[-1, 128]], compare_op=ALU.is_equal,
                                           fill=k.fillreg(e, 0.0), base=0, channel_multiplier=1), [("ones32",)], [("ident32",)])
    identb = k.sb([128, 128], BF16, "identb"); k.copy(identb[:], ident32[:])
    ones3 = k.sb([64, 8, 64], F32, "ones3"); k.memset(ones3[:], 1.0)
    identrep = k.sb([64, 8, 64], F32, "identrep")
    k.op("pool", lambda e: e.affine_select(out=identrep[:], in_=ones3[:], pattern=[[0, 8], [-1, 64]], compare_op=ALU.is_equal,
                                           fill=k.fillreg(e, 0.0), base=0, channel_multiplier=1), [("ones3",)], [("identrep",)])


    def sgu_part():
        Tq = min(512, L); nq = Tq // 128
        sp_ = k.sb([128, 8], F32, "sp_"); k.dma(sp_[:], spar)
        wsm = k.sb([128, 4, 128], F32, "wsm"); k.dma(wsm[:], swT)
        k.op("pool", lambda e: e.affine_select(out=wsm[:], in_=wsm[:], pattern=[[0, 4], [1, 128]], compare_op=ALU.is_ge,
                                               fill=k.fillreg(e, 0.0), base=0, channel_multiplier=-1), [("wsm",)], [("wsm",)])
        bsb = k.sb([128, 4, 128], F32, "bsb"); k.dma(bsb[:], sbs)
        uvt = k.sb([128, 8, Tq], F32, "uvt")
        vsq = k.sb([128, 4, Tq], F32, "vsq")
        vn = k.sb([128, 4, Tq], F32, "vn")
        mean = k.sb([128, Tq], F32, "mean"); msq = k.sb([128, Tq], F32, "msq"); srs = k.sb([128, Tq], F32, "srs")
        vtok = [k.sb([128, nq, 128], F32, "vtok%d" % i) for i in range(2)]
        yco = [k.sb([128, Tq], F32, "yco%d" % i) for i in range(2)]
        for t in range(L // Tq):
            sl = slice(t * Tq, (t + 1) * Tq)
            k.dma(uvt[:], uvsrc(sl))
            for cc in range(8):
                k.act(uvt[:, cc, :], uvt[:, cc, :], AF.Gelu_apprx_tanh, rk=[("uvt",)], wk=[("uvt", cc)])
            for cc in range(4):
                k.act(vsq[:, cc, :], uvt[:, 4 + cc, :], AF.Square, rk=[("uvt", 4 + cc)], wk=[("vsq", cc)])
            pm = nps(); pq = nps()
            for cc in range(4):
                k.mm(pm[:, :Tq], ones32[:], uvt[:, 4 + cc, :], start=(cc == 0), stop=(cc == 3))
            for cc in range(4):
                k.mm(pq[:, :Tq], ones32[:], vsq[:, cc, :], start=(cc == 0), stop=(cc == 3))
            k.act(mean[:], pm[:, :Tq], AF.Copy, scale=1.0 / 512)
            k.tt(msq[:], mean[:], mean[:], ALU.mult)
            k.stt(srs[:], pq[:, :Tq], 1.0 / 512, msq[:], ALU.mult, ALU.subtract)
            k.act(srs[:], srs[:], AF.Sqrt, bias=EPS)
            k.op("dve", lambda e: e.reciprocal(out=srs[:], in_=srs[:]), [("srs",)], [("srs",)])
            for cc in range(4):
                k.tt(vn[:, cc, :], uvt[:, 4 + cc, :], mean[:], ALU.subtract, wk=[("vn", cc)])
                k.tt(vn[:, cc, :], vn[:, cc, :], srs[:], ALU.mult, rk=[("vn", cc), ("srs",)], wk=[("vn", cc)])
                k.ts(vn[:, cc, :], vn[:, cc, :], sp_[:, cc:cc + 1], ALU.mult, sp_[:, 4 + cc:5 + cc], ALU.add,
                     rk=[("vn", cc)], wk=[("vn", cc)])
            for g in range(4):
                px = nps()
                for n in range(nq):
                    k.mm(px[:, n * 128:(n + 1) * 128], vn[:, g, n * 128:(n + 1) * 128], ident32[:], rk=[("vn", g), ("ident32",)])
                vt = vtok[g % 2]
                k.act(vt[:].rearrange("p n d -> p (n d)"), px[:, :Tq], AF.Copy)
                py = nps()
                for n in range(nq):
                    k.mm(py[:, n * 128:(n + 1) * 128], vt[:, n, :], wsm[:, g, :])
                yo_ = yco[g % 2]
                k.tt(yo_[:].rearrange("p (n t) -> p n t", n=nq), py[:, :Tq].rearrange("p (n t) -> p n t", n=nq),
                     bsb[:, g, :].unsqueeze(1).broadcast_to([128, nq, 128]), ALU.add)
                k.tt(yo_[:], yo_[:], uvt[:, g, :], ALU.mult)
                k.dma(Yf(8 + g)[:, sl], yo_[:])


    def head(j):
        if "c" in B_PARTS:
            Tb = min(512, L)
            scp = k.sb([128, 4], F32, "scp"); k.dma(scp[:], scw[j])
            mh = k.sb([128, Tb + 2], F32, "mh"); k.memset(mh[:, 0:2], 0.0)
            sA = [k.sb([128, Tb], F32, "sA%d" % i) for i in range(2)]
            sBt = [k.sb([128, Tb], F32, "sB%d" % i) for i in range(2)]
            sC = [k.sb([128, Tb], F32, "sC%d" % i) for i in range(2)]
            so = [k.sb([128, Tb], F32, "so%d" % i) for i in range(2)]
            for t in range(L // Tb):
                a_, b_, c_, o_ = sA[t % 2], sBt[t % 2], sC[t % 2], so[t % 2]
                sl = slice(t * Tb, (t + 1) * Tb)
                k.dma(a_[:], PTf(36 + j)[:, sl]); k.dma(b_[:], PTf(40 + j)[:, sl]); k.dma(c_[:], PTf(32 + j)[:, sl])
                k.tt(mh[:, 2:], a_[:], b_[:], ALU.mult)
                k.ts(o_[:], mh[:, 2:], scp[:, 2:3], ALU.mult)
                k.stt(o_[:], mh[:, 1:Tb + 1], scp[:, 1:2], o_[:], ALU.mult, ALU.add)
                k.stt(o_[:], mh[:, 0:Tb], scp[:, 0:1], o_[:], ALU.mult, ALU.add)
                k.tt(o_[:], o_[:], c_[:], ALU.mult)
                k.dma(Yf(12 + j)[:, sl], o_[:])
                k.copy(a_[:, 0:2], mh[:, Tb:Tb + 2], eng="pool")
                k.copy(mh[:, 0:2], a_[:, 0:2], eng="pool")

        if "l" in B_PARTS:
            Tb = min(512, L)
            lp = k.sb([128, 8], F32, "lp"); k.dma(lp[:], lpar[j])
            lwa = k.sb([128, 128], F32, "lwa"); k.dma(lwa[:], lw[j, 0])
            lwx = k.sb([128, 128], F32, "lwx"); k.dma(lwx[:], lw[j, 1])
            lc = k.sb([128, 4], F32, "lc")
            k.act(lc[:, 0:1], lp[:, 7:8], AF.Exp, scale=-1.0)
            k.act(lc[:, 0:1], lc[:, 0:1], AF.Ln, bias=1.0)
            k.ts(lc[:, 1:2], lc[:, 0:1], -8.0, ALU.mult)
            k.ts(lc[:, 2:3], lc[:, 0:1], -16.0, ALU.mult)
            xh = k.sb([128, Tb + 3], F32, "xh"); k.memset(xh[:, 0:3], 0.0)
            hprev = k.sb([128, 1], F32, "hprev"); k.memset(hprev[:], 0.0)
            tmp3 = k.sb([128, 4], F32, "tmp3")
            lG = [k.sb([128, Tb], F32, "lG%d" % i) for i in range(2)]
            xc = k.sb([128, Tb], F32, "xc")
            rr = k.sb([128, Tb], F32, "rr"); ii = k.sb([128, Tb], F32, "ii")
            aa = k.sb([128, Tb], F32, "aa"); a2 = k.sb([128, Tb], F32, "a2")
            hh = [k.sb([128, Tb], F32, "hh%d" % i) for i in range(2)]
            for t in range(L // Tb):
                sl = slice(t * Tb, (t + 1) * Tb)
                G = lG[t % 2]; h_ = hh[t % 2]
                k.dma(xh[:, 3:], PTf(16 + j)[:, sl]); k.dma(G[:], PTf(20 + j)[:, sl])
                k.ts(xc[:], xh[:, 3:], lp[:, 3:4], ALU.mult, lp[:, 4:5], ALU.add)
                for tap in range(3):
                    k.stt(xc[:], xh[:, tap:tap + Tb], lp[:, tap:tap + 1], xc[:], ALU.mult, ALU.add)
                k.copy(tmp3[:, 0:3], xh[:, Tb:Tb + 3], eng="pool")
                k.copy(xh[:, 0:3], tmp3[:, 0:3], eng="pool")
                for hf in range(Tb // 512 if Tb >= 512 else 1):
                    W = min(512, Tb)
                    cs = slice(hf * W, (hf + 1) * W)
                    p1 = nps(); k.mm(p1[:, :W], lwa[:], xc[:, cs])
                    k.act(rr[:, cs], p1[:, :W], AF.Sigmoid, bias=lp[:, 5:6])
                    p2 = nps(); k.mm(p2[:, :W], lwx[:], xc[:, cs])
                    k.act(ii[:, cs], p2[:, :W], AF.Sigmoid, bias=lp[:, 6:7])
                k.act(aa[:], rr[:], AF.Exp, scale=lc[:, 1:2])
                k.act(a2[:], rr[:], AF.Exp, scale=lc[:, 2:3])
                k.ts(a2[:], a2[:], -1.0, ALU.mult, 1.0, ALU.add)
                k.act(a2[:], a2[:], AF.Sqrt)
                k.tt(ii[:], ii[:], xc[:], ALU.mult)
                k.tt(ii[:], ii[:], a2[:], ALU.mult)
                k.op("dve", lambda e, h_=h_: e.tensor_tensor_scan(out=h_[:], data0=aa[:], data1=ii[:], initial=hprev[:, 0:1],
                                                                  op0=ALU.mult, op1=ALU.add),
                     [("aa",), ("ii",), ("hprev",)], [_key(h_[:])])
                k.copy(hprev[:], h_[:, Tb - 1:Tb], eng="pool")
                k.act(G[:], G[:], AF.Gelu_apprx_tanh)
                k.tt(h_[:], h_[:], G[:], ALU.mult)
                k.dma(Yf(4 + j)[:, sl], h_[:])

        if "g" in B_PARTS:
            gp = k.sb([128, 16], F32, "gp"); k.dma(gp[:], gpar[j])
            braw = k.sb([128, 64], F32, "braw"); araw = k.sb([128, 64], F32, "araw")
            k.dma(braw[0:nC, :], PBAf(j).rearrange("(n c) -> n c", c=64))
            k.dma(araw[0:nC, :], PBAf(4 + j).rearrange("(n c) -> n c", c=64))
            beta = k.sb([128, 64], F32, "beta"); gx = k.sb([128, 64], F32, "gx"); gt1 = k.sb([128, 64], F32, "gt1")
            gcum = k.sb([128, 64], F32, "gcum"); s1 = k.sb([128, 64], F32, "s1"); s3 = k.sb([128, 64], F32, "s3")
            ones64 = k.sb([128, 64], F32, "ones64"); k.memset(ones64[:], 1.0)
            nexpA = k.sb([128, 1], F32, "nexpA")
            k.act(nexpA[:], gp[:, 13:14], AF.Exp)
            k.ts(nexpA[:], nexpA[:], -1.0, ALU.mult)
            k.act(beta[0:nC, :], braw[0:nC, :], AF.Sigmoid)
            k.ts(gx[0:nC, :], araw[0:nC, :], gp[0:nC, 14:15], ALU.add)
            k.act(gt1[0:nC, :], gx[0:nC, :], AF.Abs)
            k.act(gt1[0:nC, :], gt1[0:nC, :], AF.Exp, scale=-1.0)
            k.act(gt1[0:nC, :], gt1[0:nC, :], AF.Ln, bias=1.0)
            k.ts(gx[0:nC, :], gx[0:nC, :], 0.0, ALU.max)
            k.tt(gx[0:nC, :], gx[0:nC, :], gt1[0:nC, :], ALU.add)
            k.ts(gx[0:nC, :], gx[0:nC, :], nexpA[0:nC, 0:1], ALU.mult)
            k.op("dve", lambda e: e.tensor_tensor_scan(out=gcum[0:nC, :], data0=ones64[0:nC, :], data1=gx[0:nC, :], initial=0.0,
                                                       op0=ALU.mult, op1=ALU.add), [("ones64",), ("gx",)], [("gcum",)])
            k.act(s1[0:nC, :], gcum[0:nC, :], AF.Exp)
            k.tt(s1[0:nC, :], s1[0:nC, :], beta[0:nC, :], ALU.mult)
            k.act(s3[0:nC, :], gcum[0:nC, :], AF.Exp, scale=-1.0, bias=gcum[0:nC, 63:64])
            colT = k.sb([64, 4, 128], F32, "colT")
            pc = nps()
            for wi, src in enumerate((gcum, beta, s1, s3)):
                k.mm(pc[0:64, wi * 128:wi * 128 + nC], src[0:nC, :], ident32[0:nC, 0:nC])
            for wi in range(4):
                k.act(colT[:, wi, 0:nC], pc[0:64, wi * 128:wi * 128 + nC], AF.Copy)

            GT = 512
            ng = L // GT
            qh = k.sb([128, 3, GT + 3], F32, "qh"); k.memset(qh[:, :, 0:3], 0.0)
            tmpq = k.sb([128, 3, 4], F32, "tmpq")
            zt = [k.sb([128, GT], F32, "zt%d" % i) for i in range(2)]
            cs_ = [k.sb([128, GT], F32, "cs%d" % i) for i in range(3)]
            Qb = k.sb([128, GT], BF16, "Qb"); Kb = k.sb([128, GT], BF16, "Kb"); Vb = k.sb([128, GT], BF16, "Vb")
            KBb = k.sb([128, GT], BF16, "KBb"); qg = k.sb([128, GT], BF16, "qg")
            rq = k.sb([128, GT], F32, "rq")
            rhsG = k.sb([64, 8, 64], F32, "rhsG"); rhsB = k.sb([64, 8, 64], F32, "rhsB")
            egbc = k.sb([128, GT], F32, "egbc")
            argU = k.sb([64, 8, 64], F32, "argU"); argL = k.sb([64, 8, 64], F32, "argL")
            decU = k.sb([64, 8, 64], F32, "decU"); decUs = k.sb([64, 8, 64], F32, "decUs"); decLs = k.sb([64, 8, 64], F32, "decLs")
            attnT = k.sb([64, GT], BF16, "attnT")
            Pb = [k.sb([64, GT], BF16, "Pb%d" % i) for i in range(2)]
            PTb = [k.sb([64, GT], BF16, "PTb%d" % i) for i in range(2)]
            TT32 = k.sb([64, GT], F32, "TT32"); TTb = k.sb([64, GT], BF16, "TTb")
            kbg = k.sb([64, 8, 128], BF16, "kbg"); kg = k.sb([64, 8, 128], BF16, "kg"); vbt = k.sb([64, 8, 128], BF16, "vbt")
            wTb = k.sb([128, GT], BF16, "wTb"); u32 = k.sb([64, 8, 128], F32, "u32")
            S32 = k.sb([128, 128], F32, "S32"); Sb = k.sb([128, 128], BF16, "Sb")
            k.memset(S32[:], 0.0); k.memset(Sb[:], 0.0)
            vnew = [k.sb([64, 128], BF16, "vnew%d" % i) for i in range(2)]
            o32 = k.sb([128, GT], F32, "o32")
            yat = [k.sb([128, GT], F32, "yat%d" % i) for i in range(2)]
            r3 = lambda ap: ap.rearrange("p (c j) -> p c j", c=8)

            for gi in range(ng):
                sl = slice(gi * GT, (gi + 1) * GT)
                c0 = gi * 8
                z_ = zt[gi % 2]
                for wq in range(3):
                    k.dma(qh[:, wq, 3:], PTf(4 * wq + j)[:, sl], wk=[("qh", wq)])
                k.dma(z_[:], PTf(12 + j)[:, sl])
                for wq in range(3):
                    o_ = cs_[wq]
                    k.ts(o_[:], qh[:, wq, 3:], gp[:, 4 * wq + 3:4 * wq + 4], ALU.mult, rk=[("qh", wq)])
                    for tap in range(3):
                        k.stt(o_[:], qh[:, wq, tap:tap + GT], gp[:, 4 * wq + tap:4 * wq + tap + 1], o_[:], ALU.mult, ALU.add,
                              rk=[("qh", wq), _key(o_[:])])
                    k.copy(tmpq[:, wq, 0:3], qh[:, wq, GT:GT + 3], eng="pool", rk=[("qh", wq)], wk=[("tmpq", wq)])
                    k.copy(qh[:, wq, 0:3], tmpq[:, wq, 0:3], eng="pool", rk=[("tmpq", wq)], wk=[("qh", wq)])
                    k.act(o_[:], o_[:], AF.Silu)
                r = c.rstd_of([cs_[0][:]], GT, 1.0, out=rq)
                k.stt(cs_[0][:], cs_[0][:], 128.0 ** -0.5, r[:], ALU.mult, ALU.mult)
                k.copy(Qb[:], cs_[0][:], eng="pool")
                r = c.rstd_of([cs_[1][:]], GT, 1.0, out=rq)
                k.tt(cs_[1][:], cs_[1][:], r[:], ALU.mult)
                k.copy(Kb[:], cs_[1][:], eng="pool")
                k.copy(Vb[:], cs_[2][:], eng="pool")
                GTrep = colT[:, 0, c0:c0 + 8].unsqueeze(2).broadcast_to([64, 8, 64])
                BTrep = colT[:, 1, c0:c0 + 8].unsqueeze(2).broadcast_to([64, 8, 64])
                k.tt(rhsG[:], identrep[:], GTrep, ALU.mult)
                k.tt(rhsB[:], identrep[:], BTrep, ALU.mult)
                pG = nps(); k.mm(pG[:, :], ones32[0:64, :], rhsG[:].rearrange("p c j -> p (c j)"))
                pB = nps(); k.mm(pB[:, :], ones32[0:64, :], rhsB[:].rearrange("p c j -> p (c j)"))
                k.act(egbc[:], pG[:, :], AF.Exp)
                k.tt(argU[:], r3(pG[0:64, :]), GTrep, ALU.subtract)
                k.op("pool", lambda e: e.affine_select(out=argU[:], in_=argU[:], pattern=[[0, 8], [1, 64]], compare_op=ALU.is_ge,
                                                       fill=k.fillreg(e, -30000.0), base=0, channel_multiplier=-1), [("argU",)], [("argU",)])
                k.act(decU[:], argU[:], AF.Exp)
                k.op("pool", lambda e: e.affine_select(out=decUs[:], in_=decU[:], pattern=[[0, 8], [1, 64]], compare_op=ALU.is_gt,
                                                       fill=k.fillreg(e, 0.0), base=0, channel_multiplier=-1), [("decU",)], [("decUs",)])
                k.tt(argL[:], GTrep, r3(pG[0:64, :]), ALU.subtract, rk=[("colT",), _key(pG[:])])
                k.op("pool", lambda e: e.affine_select(out=argL[:], in_=argL[:], pattern=[[0, 8], [-1, 64]], compare_op=ALU.is_gt,
                                                       fill=k.fillreg(e, -30000.0), base=0, channel_multiplier=1), [("argL",)], [("argL",)])
                k.act(decLs[:], argL[:], AF.Exp)
                k.tt(qg[:], cs_[0][:], egbc[:], ALU.mult)
                k.tt(KBb[:], cs_[1][:], pB[:, :], ALU.mult)
                pA = nps(); pMT = nps(); pM = nps()
                for ci in range(8):
                    cl = slice(ci * 64, ci * 64 + 64)
                    k.mm(pA[0:64, cl], Kb[:, cl], Qb[:, cl])
                    k.mm(pMT[0:64, cl], Kb[:, cl], KBb[:, cl])
                    k.mm(pM[0:64, cl], KBb[:, cl], Kb[:, cl])
                k.tt(attnT[:], pA[0:64, :], decU[:].rearrange("p c j -> p (c j)"), ALU.mult)
                k.stt(PTb[0][:], pMT[0:64, :], -1.0, decUs[:].rearrange("p c j -> p (c j)"), ALU.mult, ALU.mult)
                k.stt(Pb[0][:], pM[0:64, :], -1.0, decLs[:].rearrange("p c j -> p (c j)"), ALU.mult, ALU.mult)
                k.stt(TT32[:], pMT[0:64, :], -1.0, decUs[:].rearrange("p c j -> p (c j)"), ALU.mult, ALU.mult)
                k.tt(TT32[:], TT32[:], identrep[:].rearrange("p c j -> p (c j)"), ALU.add)
                k.copy(TTb[:], TT32[:], eng="pool")
                cur = 0
                for lv in range(1, 6):
                    nxt = cur ^ 1
                    pP = nps()
                    for ci in range(8):
                        cl = slice(ci * 64, ci * 64 + 64)
                        k.mm(pP[0:64, cl], PTb[cur][:, cl], Pb[cur][:, cl])
                    k.act(Pb[nxt][:], pP[0:64, :], AF.Copy)
                    if lv < 5:
                        pPT = nps()
                        for ci in range(8):
                            cl = slice(ci * 64, ci * 64 + 64)
                            k.mm(pPT[0:64, cl], Pb[cur][:, cl], PTb[cur][:, cl])
                        k.act(PTb[nxt][:], pPT[0:64, :], AF.Copy)
                    pT_ = nps()
                    for ci in range(8):
                        cl = slice(ci * 64, ci * 64 + 64)
                        k.mm(pT_[0:64, cl], Pb[nxt][:, cl], TTb[:, cl])
                    k.tt(TT32[:], TT32[:], pT_[0:64, :], ALU.add)
                    k.copy(TTb[:], TT32[:], eng="pool")
                    cur = nxt
                for hf in range(2):
                    pK = nps(); pV = nps()
                    for cj in range(4):
                        ci = hf * 4 + cj
                        cl = slice(ci * 64, ci * 64 + 64)
                        k.mm(pK[0:64, cj * 128:(cj + 1) * 128], Kb[:, cl], identb[:])
                        k.mm(pV[0:64, cj * 128:(cj + 1) * 128], Vb[:, cl], identb[:])
                    cc0 = c0 + hf * 4
                    S1rep = colT[:, 2, cc0:cc0 + 4].unsqueeze(2).broadcast_to([64, 4, 128])
                    S3rep = colT[:, 3, cc0:cc0 + 4].unsqueeze(2).broadcast_to([64, 4, 128])
                    Brep = colT[:, 1, cc0:cc0 + 4].unsqueeze(2).broadcast_to([64, 4, 128])
                    pK3 = pK[0:64, :].rearrange("p (c d) -> p c d", c=4)
                    pV3 = pV[0:64, :].rearrange("p (c d) -> p c d", c=4)
                    k.tt(kbg[:, hf * 4:hf * 4 + 4, :], pK3, S1rep, ALU.mult, wk=[("kbg", hf)])
                    k.tt(kg[:, hf * 4:hf * 4 + 4, :], pK3, S3rep, ALU.mult, wk=[("kg", hf)])
                    k.tt(vbt[:, hf * 4:hf * 4 + 4, :], pV3, Brep, ALU.mult, wk=[("vbt", hf)])
                pW = nps()
                for ci in range(8):
                    cl = slice(ci * 64, ci * 64 + 64)
                    k.mm(pW[:, cl], kbg[:, ci, :], TTb[:, cl])
                k.act(wTb[:], pW[:, :], AF.Copy)
                for hf in range(2):
                    pU = nps()
                    for cj in range(4):
                        ci = hf * 4 + cj
                        cl = slice(ci * 64, ci * 64 + 64)
                        k.mm(pU[0:64, cj * 128:(cj + 1) * 128], TTb[:, cl], vbt[:, ci, :])
                    k.act(u32[:, hf * 4:hf * 4 + 4, :].rearrange("p c d -> p (c d)"), pU[0:64, :], AF.Copy, wk=[("u32", hf)])
                for ci in range(8):
                    cl = slice(ci * 64, ci * 64 + 64)
                    vn_ = vnew[ci % 2]
                    pR = nps()
                    k.mm(pR[0:64, 0:128], wTb[:, cl], Sb[:])
                    k.tt(vn_[:], u32[:, ci, :], pR[0:64, 0:128], ALU.subtract)
                    k.mm(psO[:, cl], Sb[:], qg[:, cl], start=True, stop=False)
                    k.mm(psO[:, cl], vn_[:], attnT[:, cl], start=False, stop=True)
                    k.mm(pR[:, 128:256], kg[:, ci, :], vn_[:])
                    k.stt(S32[:], S32[:], egbc[:, ci * 64 + 63:ci * 64 + 64], pR[:, 128:256], ALU.mult, ALU.add)
                    k.act(Sb[:], S32[:], AF.Copy)
                k.act(o32[:], psO[:, :], AF.Copy)
                r = c.rstd_of([o32[:]], GT, 128.0, out=rq)
                y_ = yat[gi % 2]
                k.stt(y_[:], o32[:], gp[:, 12:13], r[:], ALU.mult, ALU.mult)
                k.act(z_[:], z_[:], AF.Silu)
                k.tt(y_[:], y_[:], z_[:], ALU.mult)
                k.dma(Yf(j)[:, sl], y_[:])


    k.prefix = base_prefix + "sg_"
    if "s" in B_PARTS:
        with k.scope():
            sgu_part()
        k.barrier()
    for j in range(4 if "h" in B_PARTS else 0):
        k.prefix = base_prefix + "h%d_" % j
        with k.scope():
            head(j)
        k.barrier()
    k.prefix = base_prefix


def build_fused(L, depth):
    nc = bass.Bass("TRN2", target_bir_lowering=False)
    c = Ctx(nc); k = c.k
    xT = c.dram("xT", [128, 16, L])
    xo = c.dram("xo", [128, 16, L], kind="ExternalOutput")
    PT = nc.dram_tensor("PT", [44, 128, L], F32)
    PBA = nc.dram_tensor("PBA", [8, L], F32)
    Y = nc.dram_tensor("Y", [16, 128, L], F32)
    PTa = PT.ap(); PBAa = PBA.ap(); Ya = Y.ap()
    uv_view = PTa[24:32].rearrange("c p t -> p c t")
    Y_view = Ya.rearrange("c p t -> p c t")
    wsrc = {}
    pending = []
    for l in range(depth):
        sfx = "_%d" % l
        for nm, shp in (("w", [11, 128, 16, 512]), ("wout", [8, 128, 16, 256]), ("wup", [44, 128, 16, 256]),
                        ("wdn", [16, 128, 44, 128])):
            src = c.dram(nm + sfx, shp)
            dstt = nc.dram_tensor(nm + "b" + sfx, shp, BF16)
            k.sbnames.add(dstt.ap().name)
            wsrc[nm + sfx] = (src, dstt.ap())
            pending.append((src, dstt.ap(), shp))
    k.prefix = "PRO_"
    with k.scope():
        stg = [k.sb([128, 8192], BF16, "stg%d" % i) for i in range(4)]
        si = 0
        for src, dst, shp in pending:
            per = shp[2] * shp[3]
            for i in range(shp[0]):
                t_ = stg[si % 4]; si += 1
                k.dma(t_[:, :per], src[i].rearrange("p a b -> p (a b)"), eng="pool")
                k.dma(dst[i].rearrange("p a b -> p (a b)"), t_[:, :per], wk=[(dst.name, i)])
    k.barrier()
    for l in range(depth):
        sfx = "_%d" % l
        gam = c.dram("gam" + sfx, [128, 16]); w = wsrc["w" + sfx][1]; wba = c.dram("wba" + sfx, [128, 16, 8])
        gpar = c.dram("gpar" + sfx, [4, 128, 16]); lpar = c.dram("lpar" + sfx, [4, 128, 8]); lw = c.dram("lw" + sfx, [4, 2, 128, 128])
        spar = c.dram("spar" + sfx, [128, 8]); swT = c.dram("swT" + sfx, [128, 4, 128]); sbs = c.dram("sbs" + sfx, [128, 4, 128])
        scw = c.dram("scw" + sfx, [4, 128, 4])
        vecs = c.dram("vecs" + sfx, [128, 60]); wout = wsrc["wout" + sfx][1]
        wup = wsrc["wup" + sfx][1]; cw = c.dram("cw" + sfx, [128, 88, 4]); wdn = wsrc["wdn" + sfx][1]
        xin = xT if l == 0 else xo
        k.prefix = "L%dA_" % l
        with k.scope():
            emit_A(c, L, xin, gam, w, wba, lambda ch: PTa[ch], PBAa, w_eng="sp", wkey=w.name)
        k.barrier()
        k.prefix = "L%dB_" % l
        with k.scope():
            emit_B(c, L, lambda ch: PTa[ch], lambda r: PBAa[r], lambda sl: uv_view[:, :, sl], lambda ch: Ya[ch],
                   gpar, lpar, lw, spar, swT, sbs, scw)
        k.barrier()
        k.prefix = "L%dC_" % l
        with k.scope():
            emit_C(c, L, lambda c0, T: xin[:, :, c0:c0 + T], lambda g, c0, T: Y_view[:, 4 * g:4 * g + 4, c0:c0 + T],
                   lambda c0, T: xo[:, :, c0:c0 + T], vecs, wout, wup, cw, wdn, w_eng="sp",
                   wkeys=(wout.name, wup.name, wdn.name))
        k.barrier()
    k.prefix = ""
    k.finish()
    k.build()
    return nc


_PROGS = {}
_PERM = np.concatenate([np.arange(0, 2048), np.arange(2056, 5640)])


def _fm(v):
    return np.ascontiguousarray(v.reshape(-1, 128).T)


def host_weights(P, l):
    d = {}
    sfx = "_%d" % l
    W = P["w_in"][l]
    d["w"] = np.ascontiguousarray(W[:, _PERM].reshape(16, 128, 11, 512).transpose(2, 1, 0, 3))
    d["wba"] = np.ascontiguousarray(W[:, 2048:2056].reshape(16, 128, 8).transpose(1, 0, 2))
    d["gam"] = _fm(P["pre_mix_norm"][l])
    gpar = np.zeros((4, 128, 16), np.float32)
    lpar = np.zeros((4, 128, 8), np.float32)
    lw = np.zeros((4, 2, 128, 128), np.float32)
    scw = np.zeros((4, 128, 4), np.float32)
    for j in range(4):
        cs = slice(j * 128, (j + 1) * 128)
        for wq in range(3):
            gpar[j, :, 4 * wq:4 * wq + 4] = P["gdn_conv_w"][l][:, wq * 512 + j * 128: wq * 512 + (j + 1) * 128].T
        gpar[j, :, 12] = P["gdn_norm_w"][l]
        gpar[j, :, 13] = P["gdn_a_log"][l][j]
        gpar[j, :, 14] = P["gdn_dt_bias"][l][j]
        lpar[j, :, 0:4] = P["lru_conv_w"][l][:, cs].T
        lpar[j, :, 4] = P["lru_conv_b"][l][cs]
        lpar[j, :, 5] = P["lru_ba"][l].reshape(-1)[cs]
        lpar[j, :, 6] = P["lru_bx"][l].reshape(-1)[cs]
        lpar[j, :, 7] = P["lru_lambda"][l][cs]
        for q, nm in enumerate(("lru_wa", "lru_wx")):
            lw[j, q, 0:64, 0:64] = P[nm][l][2 * j]
            lw[j, q, 64:128, 64:128] = P[nm][l][2 * j + 1]
        scw[j, :, 0:3] = P["sconv_w"][l][:, cs].T
    d["gpar"] = gpar; d["lpar"] = lpar; d["lw"] = lw; d["scw"] = scw
    d["spar"] = np.concatenate([_fm(P["sgu_ln_w"][l]), _fm(P["sgu_ln_b"][l])], axis=1)
    d["swT"] = np.ascontiguousarray(P["sgu_ws"][l].transpose(2, 0, 1))
    d["sbs"] = np.ascontiguousarray(np.broadcast_to(P["sgu_b"][l][None], (128, 4, 128)))
    vecs = np.zeros((128, 60), np.float32)
    for g in range(3):
        vecs[:, 4 * g:4 * g + 4] = _fm(P["grp_norm_w"][l][g])
    vecs[:, 12:28] = _fm(P["post_mix_norm"][l])
    vecs[:, 28:44] = _fm(P["pre_ffn_norm"][l])
    vecs[:, 44:60] = _fm(P["post_ffn_norm"][l])
    d["vecs"] = vecs
    d["wout"] = np.ascontiguousarray(P["w_out"][l].reshape(16, 128, 8, 256).transpose(2, 1, 0, 3))
    up = P["ffn_up"][l]
    wg = up[:, :5632].reshape(16, 128, 44, 128)
    wv = up[:, 5632:].reshape(16, 128, 44, 128)
    d["wup"] = np.ascontiguousarray(np.stack([wg, wv], axis=3).transpose(2, 1, 0, 3, 4).reshape(44, 128, 16, 256))
    cw = np.zeros((128, 88, 4), np.float32)
    cwf = P["ffn_conv_w"][l]
    cbf = P["ffn_conv_b"][l]
    for gv in range(2):
        blk = cwf[:, gv * 5632:(gv + 1) * 5632].reshape(3, 44, 128)
        cw[:, gv::2, 0:3] = blk.transpose(2, 1, 0)
        cw[:, gv::2, 3] = cbf[gv * 5632:(gv + 1) * 5632].reshape(44, 128).T
    d["cw"] = cw
    d["wdn"] = np.ascontiguousarray(P["ffn_down"][l].reshape(44, 128, 16, 128).transpose(2, 1, 0, 3))
    return {kk + sfx: v for kk, v in d.items()}


def kernel(**inputs):
    x = np.asarray(inputs["x"], np.float32)
    P = {kk: np.asarray(v, np.float32) for kk, v in inputs.items() if kk != "x"}
    B, S, _ = x.shape
    depth = P["w_in"].shape[0]
    key = (S, depth)
    if key not in _PROGS:
        _PROGS[key] = build_fused(S, depth)
    nc = _PROGS[key]
    wts = {}
    for l in range(depth):
        wts.update(host_weights(P, l))
    maps = []
    for b in range(B):
        m = dict(wts)
        m["xT"] = np.ascontiguousarray(x[b].reshape(S, 16, 128).transpose(2, 1, 0))
        maps.append(m)
    res = run_bass_kernel_spmd(nc, maps, core_ids=list(range(B))).results
    out = np.empty((B, S, 2048), np.float32)
    for b in range(B):
        out[b] = res[b]["xo"].transpose(2, 1, 0).reshape(S, 2048)
    return out
```

```python
import numpy as np
from contextlib import ExitStack
import concourse.bass as bass
import concourse.mybir as mybir
from concourse.bass_utils import run_bass_kernel_spmd

F32 = mybir.dt.float32
BF16 = mybir.dt.bfloat16
AF = mybir.ActivationFunctionType
ALU = mybir.AluOpType

EPS = 1e-6
DM = 2048
NCORES = 8
SAME_ENGINE_SYNC = True
N_DMA_SLOTS = 16


_ALIAS = {}
B_PARTS = "shclg"
SCAN_W = 128


def _key(ap):
    n = ap.name
    return (_ALIAS.get(n, n),)


class KB:
    ENGS = ("pe", "act", "dve", "pool", "sp")

    def __init__(self, nc):
        self.nc = nc
        self.es = ExitStack()
        self.items = {e: [] for e in self.ENGS}
        self.seq = {e: 0 for e in self.ENGS}
        self.sem = {}
        for e in ("pe", "act", "dve", "pool"):
            self.sem[e] = self.es.enter_context(nc.semaphore("s_" + e))
        self.dsem = [self.es.enter_context(nc.semaphore("d%d" % i)) for i in range(N_DMA_SLOTS)]
        self.dcount = [0] * N_DMA_SLOTS
        self.dnext = 0
        self.dnextq = [0, 0]
        self.last_w = {}
        self.readers = {}
        self.subs = {}
        self.waited = {e: {} for e in self.ENGS}
        self.ntile = 0

    prefix = ""

    def sb(self, shape, dt=F32, name=None):
        self.ntile += 1
        base = name or ("t%d" % self.ntile)
        _ALIAS[self.prefix + base] = base
        return self.es.enter_context(self.nc.sbuf_tensor(self.prefix + base, list(shape), dt))

    def ps(self, shape=(128, 512), dt=F32, name=None):
        self.ntile += 1
        base = name or ("p%d" % self.ntile)
        _ALIAS[self.prefix + base] = base
        return self.es.enter_context(self.nc.psum_tensor(self.prefix + base, list(shape), dt))

    def _expand(self, key):
        name = key[0]
        if len(key) == 1:
            return [key] + [(name, s) for s in self.subs.get(name, ())]
        self.subs.setdefault(name, set()).add(key[1])
        return [key, (name,)]

    def _need(self, eng, tok, waits):
        if tok is None:
            return
        if tok[0] == "c" and tok[1] == eng:
            if eng in ("pe", "sp") or not SAME_ENGINE_SYNC:
                return
        key = tok[:2]
        if self.waited[eng].get(key, 0) >= tok[2]:
            return
        self.waited[eng][key] = tok[2]
        waits.append(tok)

    def _deps(self, eng, reads, writes):
        waits = []
        for r in reads:
            for kk in self._expand(r):
                self._need(eng, self.last_w.get(kk), waits)
        for w in writes:
            for kk in self._expand(w):
                self._need(eng, self.last_w.get(kk), waits)
                for tok in self.readers.get(kk, ()):
                    self._need(eng, tok, waits)
        return waits

    def _commit(self, tok, reads, writes):
        for r in reads:
            lst = self.readers.setdefault(r, [])
            lst[:] = [t for t in lst if t[:2] != tok[:2]]
            lst.append(tok)
        for w in writes:
            self.last_w[w] = tok
            self.readers[w] = []
            if len(w) == 1:
                for s in self.subs.get(w[0], ()):
                    self.last_w[(w[0], s)] = tok
                    self.readers[(w[0], s)] = []

    def op(self, eng, fn, reads=(), writes=()):
        reads = list(reads); writes = list(writes)
        waits = self._deps(eng, reads, writes)
        self.seq[eng] += 1
        tok = ("c", eng, self.seq[eng])
        self.items[eng].append((waits, fn, ("c", eng)))
        self._commit(tok, reads, writes)

    def dma(self, out, in_, eng="sp", rk=None, wk=None):
        reads = rk if rk is not None else ([_key(in_)] if in_.name in self.sbnames else [])
        writes = wk if wk is not None else ([_key(out)] if out.name in self.sbnames else [])
        half = N_DMA_SLOTS // 2
        qi = 0 if eng == "sp" else 1
        slot = qi * half + self.dnextq[qi]
        self.dnextq[qi] = (self.dnextq[qi] + 1) % half
        waits = self._deps(eng, reads, writes)
        if self.dcount[slot] > 0:
            self._need(eng, ("d", slot, self.dcount[slot]), waits)
        self.dcount[slot] += 16
        tok = ("d", slot, self.dcount[slot])
        self.items[eng].append((waits, (lambda e: e.dma_start(out=out, in_=in_)), ("d", slot)))
        self._commit(tok, reads, writes)

    def cc(self, fn, reads=(), writes=()):
        if not hasattr(self, "ccsem"):
            self.ccsem = self.es.enter_context(self.nc.semaphore("cc_sem_kb"))
            self.cccount = 0
        reads = list(reads); writes = list(writes)
        waits = self._deps("pool", reads, writes)
        if self.cccount > 0:
            self._need("pool", ("x", 0, self.cccount), waits)
        self.cccount += 1
        tok = ("x", 0, self.cccount)
        self.items["pool"].append((waits, fn, ("x", 0)))
        self._commit(tok, reads, writes)

    sbnames = None

    def act(self, out, in_, func, bias=None, scale=None, rk=None, wk=None):
        reads = list(rk) if rk is not None else [_key(in_)]
        kw = {}
        if bias is not None:
            kw["bias"] = bias
            if not isinstance(bias, (int, float)):
                reads.append(_key(bias))
        if scale is not None:
            kw["scale"] = scale
            if not isinstance(scale, (int, float)):
                reads.append(_key(scale))
        self.op("act", lambda e: e.activation(out=out, in_=in_, func=func, **kw), reads, wk if wk is not None else [_key(out)])

    def mm(self, out, lhsT, rhs, start=True, stop=True, rk=None, wk=None):
        self.op("pe", lambda e: e.matmul(out, lhsT=lhsT, rhs=rhs, start=start, stop=stop),
                rk if rk is not None else [_key(lhsT), _key(rhs)], wk if wk is not None else [_key(out)])

    def tt(self, out, in0, in1, op, eng="dve", rk=None, wk=None):
        self.op(eng, lambda e: e.tensor_tensor(out=out, in0=in0, in1=in1, op=op),
                rk if rk is not None else [_key(in0), _key(in1)], wk if wk is not None else [_key(out)])

    def ts(self, out, in0, s1, op0, s2=None, op1=None, eng="dve", rk=None, wk=None):
        reads = list(rk) if rk is not None else [_key(in0)]
        for s in (s1, s2):
            if s is not None and not isinstance(s, (int, float)):
                reads.append(_key(s))
        if op1 is None:
            fn = lambda e: e.tensor_scalar(out=out, in0=in0, scalar1=s1, scalar2=None, op0=op0)
        else:
            fn = lambda e: e.tensor_scalar(out=out, in0=in0, scalar1=s1, scalar2=s2, op0=op0, op1=op1)
        self.op(eng, fn, reads, wk if wk is not None else [_key(out)])

    def stt(self, out, in0, scalar, in1, op0, op1, rk=None, wk=None):
        reads = list(rk) if rk is not None else [_key(in0), _key(in1)]
        if not isinstance(scalar, (int, float)):
            reads.append(_key(scalar))
        self.op("dve", lambda e: e.scalar_tensor_tensor(out=out, in0=in0, scalar=scalar, in1=in1, op0=op0, op1=op1),
                reads, wk if wk is not None else [_key(out)])

    def copy(self, out, in_, eng="dve", rk=None, wk=None):
        self.op(eng, lambda e: e.tensor_copy(out=out, in_=in_), rk if rk is not None else [_key(in_)],
                wk if wk is not None else [_key(out)])

    def memset(self, ap, val, eng="pool"):
        self.op(eng, lambda e: e.memset(ap, val), [], [_key(ap)])

    def fillreg(self, e, val):
        if not hasattr(self, "_regs"):
            self._regs = {}
        if val not in self._regs:
            self._regs[val] = e.to_reg(val)
        return self._regs[val]

    def scope(self):
        kb = self

        class _S:
            def __enter__(s_):
                s_.outer = kb.es
                kb.es = ExitStack()
                return kb

            def __exit__(s_, *a):
                kb.es.close()
                kb.es = s_.outer
                return False
        return _S()

    def barrier(self, include_pool=False):
        toks = [("c", e, self.seq[e]) for e in ("pe", "act", "dve", "pool") if self.seq[e]]
        nsl = N_DMA_SLOTS if include_pool else N_DMA_SLOTS // 2
        toks += [("d", s_, self.dcount[s_]) for s_ in range(nsl) if self.dcount[s_]]
        for eng in self.ENGS:
            waits = []
            for tok in toks:
                self._need(eng, tok, waits)
            if waits:
                self.items[eng].append((waits, None, None))

    def finish(self):
        waits = []
        for slot in range(N_DMA_SLOTS):
            if self.dcount[slot]:
                self._need("sp", ("d", slot, self.dcount[slot]), waits)
        for e in ("pe", "act", "dve", "pool"):
            if self.seq[e]:
                self._need("sp", ("c", e, self.seq[e]), waits)
        if getattr(self, "cccount", 0):
            self._need("sp", ("x", 0, self.cccount), waits)
        self.items["sp"].append((waits, None, None))

    def build(self):
        nc = self.nc
        block = self.es.enter_context(nc.Block())

        def run(engname):
            def f(eng):
                for waits, fn, kind in self.items[engname]:
                    for tok in waits:
                        if tok[0] == "c":
                            eng.wait_ge(self.sem[tok[1]], tok[2])
                        elif tok[0] == "x":
                            eng.wait_ge(self.ccsem, tok[2])
                        else:
                            eng.wait_ge(self.dsem[tok[1]], tok[2])
                    if fn is None:
                        continue
                    ins = fn(eng)
                    if kind[0] == "c":
                        ins.then_inc(self.sem[kind[1]], 1)
                    elif kind[0] == "x":
                        ins.then_inc(self.ccsem, 1)
                    else:
                        ins.then_inc(self.dsem[kind[1]], 16)
            return f

        block.tensor(run("pe"))
        block.scalar(run("act"))
        block.vector(run("dve"))
        block.gpsimd(run("pool"))
        block.sync(run("sp"))
        self.es.close()


class Ctx:
    def __init__(self, nc):
        self.nc = nc
        self.k = KB(nc)
        k = self.k
        k.sbnames = set()
        _sb = k.sb

        def sb(shape, dt=F32, name=None):
            t = _sb(shape, dt, name)
            k.sbnames.add(t[:].name)
            return t
        k.sb = sb
        _ps = k.ps

        def ps(shape=(128, 512), dt=F32, name=None):
            t = _ps(shape, dt, name)
            k.sbnames.add(t[:].name)
            return t
        k.ps = ps
        self.ones = k.sb([128, 128], BF16, "ones")
        k.memset(self.ones[:], 1.0)
        self.sq = [k.sb([128, 512], BF16, "sq%d" % i) for i in range(2)]
        self.sqi = 0
        self.rstd = k.sb([128, 512], F32, "rstd")
        self.ps_stat = k.ps(name="ps_stat")

    def dram(self, name, shape, dt=F32, kind="ExternalInput"):
        return self.nc.dram_tensor(name, list(shape), dt, kind=kind).ap()

    def rstd_of(self, chunks, T, D, out=None, ps=None):
        k = self.k
        out = out if out is not None else self.rstd
        ps = ps if ps is not None else self.ps_stat
        n = len(chunks)
        for i, c in enumerate(chunks):
            s = self.sq[self.sqi]; self.sqi ^= 1
            k.act(s[:, :T], c, AF.Square)
            k.mm(ps[:, :T], self.ones[:], s[:, :T], start=(i == 0), stop=(i == n - 1))
        k.act(out[:, :T], ps[:, :T], AF.Sqrt, bias=EPS, scale=1.0 / D)
        k.op("dve", lambda e: e.reciprocal(out=out[:, :T], in_=out[:, :T]), [_key(out[:])], [_key(out[:])])
        return out


def emit_A(c, n_tok, xT, gam, w, wba, pT, pba, w_eng="pool", wkey=None, hook=None):
    k = c.k
    T = min(512, n_tok)
    SBT = min(2048, n_tok)
    nsb = n_tok // SBT; nt = SBT // T
    gam_sb = k.sb([128, 16], F32, "gam_sb")
    k.dma(gam_sb[:], gam)
    wba_sb = k.sb([128, 16, 8], BF16, "wba_sb")
    k.dma(wba_sb[:], wba, eng="pool")
    if hook is not None:
        hook()
    xt = [k.sb([128, 16, T], F32, "xt%d" % i) for i in range(2)]
    xn = k.sb([128, 16, SBT], BF16, "xn")
    wsl = [k.sb([128, 16, 512], BF16, "wsl%d" % i) for i in range(2)]
    osb = [k.sb([128, T], F32, "osb%d" % i) for i in range(4)]
    psb = [k.ps(name="psA%d" % i) for i in range(4)]
    cnt = 0; xi = 0; wi = 0
    for sb_ in range(nsb):
        base = sb_ * SBT
        for t in range(nt):
            x_ = xt[xi % 2]; xi += 1
            k.dma(x_[:], xT[:, :, base + t * T:base + (t + 1) * T])
            r = c.rstd_of([x_[:, kk, :] for kk in range(16)], T, DM)
            for kk in range(16):
                k.stt(xn[:, kk, t * T:(t + 1) * T], x_[:, kk, :], gam_sb[:, kk:kk + 1], r[:, :T], ALU.mult, ALU.mult,
                      wk=[("xn", t)])
        for s in range(11):
            ws = wsl[wi % 2]; wi += 1
            k.dma(ws[:], w[s], eng=w_eng, rk=([(wkey, s)] if wkey else None))
            for t in range(nt):
                for cc in range(4):
                    ch = s * 4 + cc
                    ps = psb[cnt % 4]; ob = osb[cnt % 4]
                    for kk in range(16):
                        k.mm(ps[:, :T], ws[:, kk, cc * 128:(cc + 1) * 128], xn[:, kk, t * T:(t + 1) * T],
                             start=(kk == 0), stop=(kk == 15), rk=[_key(ws[:]), ("xn", t)])
                    if cnt % 2 == 0:
                        k.act(ob[:], ps[:, :T], AF.Copy)
                    else:
                        k.copy(ob[:], ps[:, :T])
                    k.dma(pT(ch)[:, base + t * T:base + (t + 1) * T], ob[:])
                    cnt += 1
        for t in range(nt):
            ps = psb[cnt % 4]; ob = osb[cnt % 4]
            for kk in range(16):
                k.mm(ps[0:8, :T], wba_sb[:, kk, :], xn[:, kk, t * T:(t + 1) * T], start=(kk == 0), stop=(kk == 15),
                     rk=[_key(wba_sb[:]), ("xn", t)])
            k.act(ob[0:8, :], ps[0:8, :T], AF.Copy)
            k.dma(pba[:, base + t * T:base + (t + 1) * T], ob[0:8, :])
            cnt += 1


def emit_C(c, n_tok, xsrc, ysrc, xdst, vecs, wout, wup, cw, wdn, w_eng="pool", wkeys=None):
    k = c.k
    T = min(512, n_tok); nt = n_tok // T
    vec_sb = k.sb([128, 60], F32, "vec_sb"); k.dma(vec_sb[:], vecs)
    cw_sb = k.sb([128, 88, 4], F32, "cw_sb"); k.dma(cw_sb[:], cw)
    GNW, PMN, PFN, PFFN = 0, 12, 28, 44
    xt = k.sb([128, 16, T], F32, "xt")
    yo = k.sb([128, 16, T], F32, "yo")
    nb = k.sb([128, 16, T], BF16, "nb")
    actb = k.sb([128, 44, T], BF16, "actb")
    NWS, NWD = 4, 2
    wsl = [k.sb([128, 16, 256], BF16, "wsl%d" % i) for i in range(NWS)]
    wdl = [k.sb([128, 44, 128], BF16, "wdl%d" % i) for i in range(NWD)]
    hb = [k.sb([128, T + 2], F32, "hb%d" % i) for i in range(4)]
    cv = [k.sb([128, T], F32, "cv%d" % i) for i in range(4)]
    gl = [k.sb([128, T], F32, "gl%d" % i) for i in range(2)]
    halo = k.sb([128, 88, 2], F32, "halo")
    k.memset(halo[:], 0.0)
    psb = [k.ps(name="psC%d" % i) for i in range(4)]
    st = {"ps": 0, "w": 0, "wd": 0, "hb": 0}

    def tile(col0, Tt, halo_only, ocol0):
        k.dma(xt[:, :, :Tt], xsrc(col0, Tt))
        for g in range(4):
            stg = yo[:, 8 + 4 * (g % 2):12 + 4 * (g % 2), :Tt]
            skeys = [("yo", 8 + 4 * (g % 2) + i) for i in range(4)]
            k.dma(stg, ysrc(g, col0, Tt), wk=skeys)
            if g == 0:
                for cc in range(4):
                    k.act(nb[:, cc, :Tt], stg[:, cc, :], AF.Copy, rk=[skeys[cc]], wk=[("nb", cc)])
            else:
                for i in range(4):
                    s = c.sq[c.sqi]; c.sqi ^= 1
                    k.act(s[:, :Tt], stg[:, i, :], AF.Square, rk=[skeys[i]])
                    k.mm(c.ps_stat[:, :Tt], c.ones[:], s[:, :Tt], start=(i == 0), stop=(i == 3))
                k.act(c.rstd[:, :Tt], c.ps_stat[:, :Tt], AF.Sqrt, bias=EPS, scale=1.0 / 512)
                k.op("dve", lambda e: e.reciprocal(out=c.rstd[:, :Tt], in_=c.rstd[:, :Tt]), [("rstd",)], [("rstd",)])
                for cc in range(4):
                    col = GNW + (g - 1) * 4 + cc
                    k.stt(nb[:, 4 * g + cc, :Tt], stg[:, cc, :], vec_sb[:, col:col + 1], c.rstd[:, :Tt], ALU.mult, ALU.mult,
                          rk=[skeys[cc], ("rstd",)], wk=[("nb", 4 * g + cc)])
        for s in range(8):
            ws = wsl[st["w"] % NWS]; st["w"] += 1
            k.dma(ws[:], wout[s], eng=w_eng, rk=([(wkeys[0], s)] if wkeys else None))
            for cc in range(2):
                ch = 2 * s + cc
                ps = psb[st["ps"] % 4]; st["ps"] += 1
                for kk in range(16):
                    k.mm(ps[:, :Tt], ws[:, kk, cc * 128:(cc + 1) * 128], nb[:, kk, :Tt], start=(kk == 0), stop=(kk == 15))
                k.act(yo[:, ch, :Tt], ps[:, :Tt], AF.Copy, wk=[("yo", ch)])
        r = c.rstd_of([yo[:, kk, :Tt] for kk in range(16)], Tt, DM)
        for kk in range(16):
            k.stt(yo[:, kk, :Tt], yo[:, kk, :Tt], vec_sb[:, PMN + kk:PMN + kk + 1], r[:, :Tt], ALU.mult, ALU.mult,
                  rk=[("yo", kk), ("rstd",)], wk=[("yo", kk)])
            k.tt(xt[:, kk, :Tt], xt[:, kk, :Tt], yo[:, kk, :Tt], ALU.add, rk=[("xt", kk), ("yo", kk)], wk=[("xt", kk)])
        r = c.rstd_of([xt[:, kk, :Tt] for kk in range(16)], Tt, DM)
        for kk in range(16):
            k.stt(nb[:, kk, :Tt], xt[:, kk, :Tt], vec_sb[:, PFN + kk:PFN + kk + 1], r[:, :Tt], ALU.mult, ALU.mult,
                  rk=[("xt", kk), ("rstd",)], wk=[("nb", kk)])
        for j in range(44):
            ws = wsl[st["w"] % NWS]; st["w"] += 1
            k.dma(ws[:], wup[j], eng=w_eng, rk=([(wkeys[1], j)] if wkeys else None))
            cvs = []
            for gv in range(2):
                idx = 2 * j + gv
                ps = psb[st["ps"] % 4]; st["ps"] += 1
                hbt = hb[st["hb"] % 4]; cvt = cv[st["hb"] % 4]; st["hb"] += 1
                for kk in range(16):
                    k.mm(ps[:, :Tt], ws[:, kk, gv * 128:(gv + 1) * 128], nb[:, kk, :Tt], start=(kk == 0), stop=(kk == 15))
                k.copy(hbt[:, 0:2], halo[:, idx, :], eng="pool", rk=[("halo", idx)])
                k.act(hbt[:, 2:2 + Tt], ps[:, :Tt], AF.Copy)
                k.copy(halo[:, idx, :], hbt[:, Tt:Tt + 2], eng="pool", wk=[("halo", idx)])
                if halo_only:
                    continue
                k.ts(cvt[:, :Tt], hbt[:, 2:2 + Tt], cw_sb[:, idx, 2:3], ALU.mult, cw_sb[:, idx, 3:4], ALU.add)
                k.stt(cvt[:, :Tt], hbt[:, 1:1 + Tt], cw_sb[:, idx, 1:2], cvt[:, :Tt], ALU.mult, ALU.add)
                k.stt(cvt[:, :Tt], hbt[:, 0:Tt], cw_sb[:, idx, 0:1], cvt[:, :Tt], ALU.mult, ALU.add)
                cvs.append(cvt)
            if halo_only:
                continue
            g_ = gl[j % 2]
            k.act(g_[:, :Tt], cvs[0][:, :Tt], AF.Gelu_apprx_tanh)
            k.tt(actb[:, j, :Tt], g_[:, :Tt], cvs[1][:, :Tt], ALU.mult, wk=[("actb", j)])
        if halo_only:
            return
        for ch in range(16):
            wd = wdl[st["wd"] % NWD]; st["wd"] += 1
            k.dma(wd[:], wdn[ch], eng=w_eng, rk=([(wkeys[2], ch)] if wkeys else None))
            ps = psb[st["ps"] % 4]; st["ps"] += 1
            for kk in range(44):
                k.mm(ps[:, :Tt], wd[:, kk, :], actb[:, kk, :Tt], start=(kk == 0), stop=(kk == 43))
            k.act(yo[:, ch, :Tt], ps[:, :Tt], AF.Copy, wk=[("yo", ch)])
        r = c.rstd_of([yo[:, kk, :Tt] for kk in range(16)], Tt, DM)
        for kk in range(16):
            k.stt(yo[:, kk, :Tt], yo[:, kk, :Tt], vec_sb[:, PFFN + kk:PFFN + kk + 1], r[:, :Tt], ALU.mult, ALU.mult,
                  rk=[("yo", kk), ("rstd",)], wk=[("yo", kk)])
            k.tt(xt[:, kk, :Tt], xt[:, kk, :Tt], yo[:, kk, :Tt], ALU.add, rk=[("xt", kk), ("yo", kk)], wk=[("xt", kk)])
        k.dma(xdst(ocol0, Tt), xt[:, :, :Tt])

    for t in range(nt):
        tile(t * T, T, False, t * T)


def emit_B(c, L, PTf, PBAf, uvsrc, Yf, gpar, lpar, lw, spar, swT, sbs, scw):
    k = c.k
    nC = L // 64
    assert nC <= 128
    base_prefix = k.prefix
    NB = 6
    psb = [k.ps(name="psB%d" % i) for i in range(NB)]
    psO = k.ps(name="psO")
    st = {"ps": 0}

    def nps():
        p = psb[st["ps"] % NB]; st["ps"] += 1
        return p

    ones32 = k.sb([128, 128], F32, "ones32"); k.memset(ones32[:], 1.0)
    ident32 = k.sb([128, 128], F32, "ident32")
    k.op("pool", lambda e: e.affine_select(out=ident32[:], in_=ones32[:], pattern=[[-1, 128]], compare_op=ALU.is_equal,
                                           fill=k.fillreg(e, 0.0), base=0, channel_multiplier=1), [("ones32",)], [("ident32",)])
    identb = k.sb([128, 128], BF16, "identb"); k.copy(identb[:], ident32[:])
    ones3 = k.sb([64, 8, 64], F32, "ones3"); k.memset(ones3[:], 1.0)
    identrep = k.sb([64, 8, 64], F32, "identrep")
    k.op("pool", lambda e: e.affine_select(out=identrep[:], in_=ones3[:], pattern=[[0, 8], [-1, 64]], compare_op=ALU.is_equal,
                                           fill=k.fillreg(e, 0.0), base=0, channel_multiplier=1), [("ones3",)], [("identrep",)])


    def sgu_part():
        Tq = min(512, L); nq = Tq // 128
        sp_ = k.sb([128, 8], F32, "sp_"); k.dma(sp_[:], spar)
        wsm = k.sb([128, 4, 128], F32, "wsm"); k.dma(wsm[:], swT)
        k.op("pool", lambda e: e.affine_select(out=wsm[:], in_=wsm[:], pattern=[[0, 4], [1, 128]], compare_op=ALU.is_ge,
                                               fill=k.fillreg(e, 0.0), base=0, channel_multiplier=-1), [("wsm",)], [("wsm",)])
        bsb = k.sb([128, 4, 128], F32, "bsb"); k.dma(bsb[:], sbs)
        uvt = k.sb([128, 8, Tq], F32, "uvt")
        vsq = k.sb([128, 4, Tq], F32, "vsq")
        vn = k.sb([128, 4, Tq], F32, "vn")
        mean = k.sb([128, Tq], F32, "mean"); msq = k.sb([128, Tq], F32, "msq"); srs = k.sb([128, Tq], F32, "srs")
        vtok = [k.sb([128, nq, 128], F32, "vtok%d" % i) for i in range(2)]
        yco = [k.sb([128, Tq], F32, "yco%d" % i) for i in range(2)]
        for t in range(L // Tq):
            sl = slice(t * Tq, (t + 1) * Tq)
            k.dma(uvt[:], uvsrc(sl))
            for cc in range(8):
                k.act(uvt[:, cc, :], uvt[:, cc, :], AF.Gelu_apprx_tanh, rk=[("uvt",)], wk=[("uvt", cc)])
            for cc in range(4):
                k.act(vsq[:, cc, :], uvt[:, 4 + cc, :], AF.Square, rk=[("uvt", 4 + cc)], wk=[("vsq", cc)])
            pm = nps(); pq = nps()
            for cc in range(4):
                k.mm(pm[:, :Tq], ones32[:], uvt[:, 4 + cc, :], start=(cc == 0), stop=(cc == 3))
            for cc in range(4):
                k.mm(pq[:, :Tq], ones32[:], vsq[:, cc, :], start=(cc == 0), stop=(cc == 3))
            k.act(mean[:], pm[:, :Tq], AF.Copy, scale=1.0 / 512)
            k.tt(msq[:], mean[:], mean[:], ALU.mult)
            k.stt(srs[:], pq[:, :Tq], 1.0 / 512, msq[:], ALU.mult, ALU.subtract)
            k.act(srs[:], srs[:], AF.Sqrt, bias=EPS)
            k.op("dve", lambda e: e.reciprocal(out=srs[:], in_=srs[:]), [("srs",)], [("srs",)])
            for cc in range(4):
                k.tt(vn[:, cc, :], uvt[:, 4 + cc, :], mean[:], ALU.subtract, wk=[("vn", cc)])
                k.tt(vn[:, cc, :], vn[:, cc, :], srs[:], ALU.mult, rk=[("vn", cc), ("srs",)], wk=[("vn", cc)])
                k.ts(vn[:, cc, :], vn[:, cc, :], sp_[:, cc:cc + 1], ALU.mult, sp_[:, 4 + cc:5 + cc], ALU.add,
                     rk=[("vn", cc)], wk=[("vn", cc)])
            for g in range(4):
                px = nps()
                for n in range(nq):
                    k.mm(px[:, n * 128:(n + 1) * 128], vn[:, g, n * 128:(n + 1) * 128], ident32[:], rk=[("vn", g), ("ident32",)])
                vt = vtok[g % 2]
                k.act(vt[:].rearrange("p n d -> p (n d)"), px[:, :Tq], AF.Copy)
                py = nps()
                for n in range(nq):
                    k.mm(py[:, n * 128:(n + 1) * 128], vt[:, n, :], wsm[:, g, :])
                yo_ = yco[g % 2]
                k.tt(yo_[:].rearrange("p (n t) -> p n t", n=nq), py[:, :Tq].rearrange("p (n t) -> p n t", n=nq),
                     bsb[:, g, :].unsqueeze(1).broadcast_to([128, nq, 128]), ALU.add)
                k.tt(yo_[:], yo_[:], uvt[:, g, :], ALU.mult)
                k.dma(Yf(8 + g)[:, sl], yo_[:])


    def head(j):
        if "c" in B_PARTS:
            Tb = min(512, L)
            scp = k.sb([128, 4], F32, "scp"); k.dma(scp[:], scw[j])
            mh = k.sb([128, Tb + 2], F32, "mh"); k.memset(mh[:, 0:2], 0.0)
            sA = [k.sb([128, Tb], F32, "sA%d" % i) for i in range(2)]
            sBt = [k.sb([128, Tb], F32, "sB%d" % i) for i in range(2)]
            sC = [k.sb([128, Tb], F32, "sC%d" % i) for i in range(2)]
            so = [k.sb([128, Tb], F32, "so%d" % i) for i in range(2)]
            for t in range(L // Tb):
                a_, b_, c_, o_ = sA[t % 2], sBt[t % 2], sC[t % 2], so[t % 2]
                sl = slice(t * Tb, (t + 1) * Tb)
                k.dma(a_[:], PTf(36 + j)[:, sl]); k.dma(b_[:], PTf(40 + j)[:, sl]); k.dma(c_[:], PTf(32 + j)[:, sl])
                k.tt(mh[:, 2:], a_[:], b_[:], ALU.mult)
                k.ts(o_[:], mh[:, 2:], scp[:, 2:3], ALU.mult)
                k.stt(o_[:], mh[:, 1:Tb + 1], scp[:, 1:2], o_[:], ALU.mult, ALU.add)
                k.stt(o_[:], mh[:, 0:Tb], scp[:, 0:1], o_[:], ALU.mult, ALU.add)
                k.tt(o_[:], o_[:], c_[:], ALU.mult)
                k.dma(Yf(12 + j)[:, sl], o_[:])
                k.copy(a_[:, 0:2], mh[:, Tb:Tb + 2], eng="pool")
                k.copy(mh[:, 0:2], a_[:, 0:2], eng="pool")

        if "l" in B_PARTS:
            Tb = min(512, L)
            lp = k.sb([128, 8], F32, "lp"); k.dma(lp[:], lpar[j])
            lwa = k.sb([128, 128], F32, "lwa"); k.dma(lwa[:], lw[j, 0])
            lwx = k.sb([128, 128], F32, "lwx"); k.dma(lwx[:], lw[j, 1])
            lc = k.sb([128, 4], F32, "lc")
            k.act(lc[:, 0:1], lp[:, 7:8], AF.Exp, scale=-1.0)
            k.act(lc[:, 0:1], lc[:, 0:1], AF.Ln, bias=1.0)
            k.ts(lc[:, 1:2], lc[:, 0:1], -8.0, ALU.mult)
            k.ts(lc[:, 2:3], lc[:, 0:1], -16.0, ALU.mult)
            xh = k.sb([128, Tb + 3], F32, "xh"); k.memset(xh[:, 0:3], 0.0)
            hprev = k.sb([128, 1], F32, "hprev"); k.memset(hprev[:], 0.0)
            tmp3 = k.sb([128, 4], F32, "tmp3")
            lG = [k.sb([128, Tb], F32, "lG%d" % i) for i in range(2)]
            xc = k.sb([128, Tb], F32, "xc")
            rr = k.sb([128, Tb], F32, "rr"); ii = k.sb([128, Tb], F32, "ii")
            aa = k.sb([128, Tb], F32, "aa"); a2 = k.sb([128, Tb], F32, "a2")
            hh = [k.sb([128, Tb], F32, "hh%d" % i) for i in range(2)]
            for t in range(L // Tb):
                sl = slice(t * Tb, (t + 1) * Tb)
                G = lG[t % 2]; h_ = hh[t % 2]
                k.dma(xh[:, 3:], PTf(16 + j)[:, sl]); k.dma(G[:], PTf(20 + j)[:, sl])
                k.ts(xc[:], xh[:, 3:], lp[:, 3:4], ALU.mult, lp[:, 4:5], ALU.add)
                for tap in range(3):
                    k.stt(xc[:], xh[:, tap:tap + Tb], lp[:, tap:tap + 1], xc[:], ALU.mult, ALU.add)
                k.copy(tmp3[:, 0:3], xh[:, Tb:Tb + 3], eng="pool")
                k.copy(xh[:, 0:3], tmp3[:, 0:3], eng="pool")
                for hf in range(Tb // 512 if Tb >= 512 else 1):
                    W = min(512, Tb)
                    cs = slice(hf * W, (hf + 1) * W)
                    p1 = nps(); k.mm(p1[:, :W], lwa[:], xc[:, cs])
                    k.act(rr[:, cs], p1[:, :W], AF.Sigmoid, bias=lp[:, 5:6])
                    p2 = nps(); k.mm(p2[:, :W], lwx[:], xc[:, cs])
                    k.act(ii[:, cs], p2[:, :W], AF.Sigmoid, bias=lp[:, 6:7])
                k.act(aa[:], rr[:], AF.Exp, scale=lc[:, 1:2])
                k.act(a2[:], rr[:], AF.Exp, scale=lc[:, 2:3])
                k.ts(a2[:], a2[:], -1.0, ALU.mult, 1.0, ALU.add)
                k.act(a2[:], a2[:], AF.Sqrt)
                k.tt(ii[:], ii[:], xc[:], ALU.mult)
                k.tt(ii[:], ii[:], a2[:], ALU.mult)
                k.op("dve", lambda e, h_=h_: e.tensor_tensor_scan(out=h_[:], data0=aa[:], data1=ii[:], initial=hprev[:, 0:1],
                                                                  op0=ALU.mult, op1=ALU.add),
                     [("aa",), ("ii",), ("hprev",)], [_key(h_[:])])
                k.copy(hprev[:], h_[:, Tb - 1:Tb], eng="pool")
                k.act(G[:], G[:], AF.Gelu_apprx_tanh)
                k.tt(h_[:], h_[:], G[:], ALU.mult)
                k.dma(Yf(4 + j)[:, sl], h_[:])

        if "g" in B_PARTS:
            gp = k.sb([128, 16], F32, "gp"); k.dma(gp[:], gpar[j])
            braw = k.sb([128, 64], F32, "braw"); araw = k.sb([128, 64], F32, "araw")
            k.dma(braw[0:nC, :], PBAf(j).rearrange("(n c) -> n c", c=64))
            k.dma(araw[0:nC, :], PBAf(4 + j).rearrange("(n c) -> n c", c=64))
            beta = k.sb([128, 64], F32, "beta"); gx = k.sb([128, 64], F32, "gx"); gt1 = k.sb([128, 64], F32, "gt1")
            gcum = k.sb([128, 64], F32, "gcum"); s1 = k.sb([128, 64], F32, "s1"); s3 = k.sb([128, 64], F32, "s3")
            ones64 = k.sb([128, 64], F32, "ones64"); k.memset(ones64[:], 1.0)
            nexpA = k.sb([128, 1], F32, "nexpA")
            k.act(nexpA[:], gp[:, 13:14], AF.Exp)
            k.ts(nexpA[:], nexpA[:], -1.0, ALU.mult)
            k.act(beta[0:nC, :], braw[0:nC, :], AF.Sigmoid)
            k.ts(gx[0:nC, :], araw[0:nC, :], gp[0:nC, 14:15], ALU.add)
            k.act(gt1[0:nC, :], gx[0:nC, :], AF.Abs)
            k.act(gt1[0:nC, :], gt1[0:nC, :], AF.Exp, scale=-1.0)
            k.act(gt1[0:nC, :], gt1[0:nC, :], AF.Ln, bias=1.0)
            k.ts(gx[0:nC, :], gx[0:nC, :], 0.0, ALU.max)
            k.tt(gx[0:nC, :], gx[0:nC, :], gt1[0:nC, :], ALU.add)
            k.ts(gx[0:nC, :], gx[0:nC, :], nexpA[0:nC, 0:1], ALU.mult)
            k.op("dve", lambda e: e.tensor_tensor_scan(out=gcum[0:nC, :], data0=ones64[0:nC, :], data1=gx[0:nC, :], initial=0.0,
                                                       op0=ALU.mult, op1=ALU.add), [("ones64",), ("gx",)], [("gcum",)])
            k.act(s1[0:nC, :], gcum[0:nC, :], AF.Exp)
            k.tt(s1[0:nC, :], s1[0:nC, :], beta[0:nC, :], ALU.mult)
            k.act(s3[0:nC, :], gcum[0:nC, :], AF.Exp, scale=-1.0, bias=gcum[0:nC, 63:64])
            colT = k.sb([64, 4, 128], F32, "colT")
            pc = nps()
            for wi, src in enumerate((gcum, beta, s1, s3)):
                k.mm(pc[0:64, wi * 128:wi * 128 + nC], src[0:nC, :], ident32[0:nC, 0:nC])
            for wi in range(4):
                k.act(colT[:, wi, 0:nC], pc[0:64, wi * 128:wi * 128 + nC], AF.Copy)

            GT = 512
            ng = L // GT
            qh = k.sb([128, 3, GT + 3], F32, "qh"); k.memset(qh[:, :, 0:3], 0.0)
            tmpq = k.sb([128, 3, 4], F32, "tmpq")
            zt = [k.sb([128, GT], F32, "zt%d" % i) for i in range(2)]
            cs_ = [k.sb([128, GT], F32, "cs%d" % i) for i in range(3)]
            Qb = k.sb([128, GT], BF16, "Qb"); Kb = k.sb([128, GT], BF16, "Kb"); Vb = k.sb([128, GT], BF16, "Vb")
            KBb = k.sb([128, GT], BF16, "KBb"); qg = k.sb([128, GT], BF16, "qg")
            rq = k.sb([128, GT], F32, "rq")
            rhsG = k.sb([64, 8, 64], F32, "rhsG"); rhsB = k.sb([64, 8, 64], F32, "rhsB")
            egbc = k.sb([128, GT], F32, "egbc")
            argU = k.sb([64, 8, 64], F32, "argU"); argL = k.sb([64, 8, 64], F32, "argL")
            decU = k.sb([64, 8, 64], F32, "decU"); decUs = k.sb([64, 8, 64], F32, "decUs"); decLs = k.sb([64, 8, 64], F32, "decLs")
            attnT = k.sb([64, GT], BF16, "attnT")
            Pb = [k.sb([64, GT], BF16, "Pb%d" % i) for i in range(2)]
            PTb = [k.sb([64, GT], BF16, "PTb%d" % i) for i in range(2)]
            TT32 = k.sb([64, GT], F32, "TT32"); TTb = k.sb([64, GT], BF16, "TTb")
            kbg = k.sb([64, 8, 128], BF16, "kbg"); kg = k.sb([64, 8, 128], BF16, "kg"); vbt = k.sb([64, 8, 128], BF16, "vbt")
            wTb = k.sb([128, GT], BF16, "wTb"); u32 = k.sb([64, 8, 128], F32, "u32")
            S32 = k.sb([128, 128], F32, "S32"); Sb = k.sb([128, 128], BF16, "Sb")
            k.memset(S32[:], 0.0); k.memset(Sb[:], 0.0)
            vnew = [k.sb([64, 128], BF16, "vnew%d" % i) for i in range(2)]
            o32 = k.sb([128, GT], F32, "o32")
            yat = [k.sb([128, GT], F32, "yat%d" % i) for i in range(2)]
            r3 = lambda ap: ap.rearrange("p (c j) -> p c j", c=8)

            for gi in range(ng):
                sl = slice(gi * GT, (gi + 1) * GT)
                c0 = gi * 8
                z_ = zt[gi % 2]
                for wq in range(3):
                    k.dma(qh[:, wq, 3:], PTf(4 * wq + j)[:, sl], wk=[("qh", wq)])
                k.dma(z_[:], PTf(12 + j)[:, sl])
                for wq in range(3):
                    o_ = cs_[wq]
                    k.ts(o_[:], qh[:, wq, 3:], gp[:, 4 * wq + 3:4 * wq + 4], ALU.mult, rk=[("qh", wq)])
                    for tap in range(3):
                        k.stt(o_[:], qh[:, wq, tap:tap + GT], gp[:, 4 * wq + tap:4 * wq + tap + 1], o_[:], ALU.mult, ALU.add,
                              rk=[("qh", wq), _key(o_[:])])
                    k.copy(tmpq[:, wq, 0:3], qh[:, wq, GT:GT + 3], eng="pool", rk=[("qh", wq)], wk=[("tmpq", wq)])
                    k.copy(qh[:, wq, 0:3], tmpq[:, wq, 0:3], eng="pool", rk=[("tmpq", wq)], wk=[("qh", wq)])
                    k.act(o_[:], o_[:], AF.Silu)
                r = c.rstd_of([cs_[0][:]], GT, 1.0, out=rq)
                k.stt(cs_[0][:], cs_[0][:], 128.0 ** -0.5, r[:], ALU.mult, ALU.mult)
                k.copy(Qb[:], cs_[0][:], eng="pool")
                r = c.rstd_of([cs_[1][:]], GT, 1.0, out=rq)
                k.tt(cs_[1][:], cs_[1][:], r[:], ALU.mult)
                k.copy(Kb[:], cs_[1][:], eng="pool")
                k.copy(Vb[:], cs_[2][:], eng="pool")
                GTrep = colT[:, 0, c0:c0 + 8].unsqueeze(2).broadcast_to([64, 8, 64])
                BTrep = colT[:, 1, c0:c0 + 8].unsqueeze(2).broadcast_to([64, 8, 64])
                k.tt(rhsG[:], identrep[:], GTrep, ALU.mult)
                k.tt(rhsB[:], identrep[:], BTrep, ALU.mult)
                pG = nps(); k.mm(pG[:, :], ones32[0:64, :], rhsG[:].rearrange("p c j -> p (c j)"))
                pB = nps(); k.mm(pB[:, :], ones32[0:64, :], rhsB[:].rearrange("p c j -> p (c j)"))
                k.act(egbc[:], pG[:, :], AF.Exp)
                k.tt(argU[:], r3(pG[0:64, :]), GTrep, ALU.subtract)
                k.op("pool", lambda e: e.affine_select(out=argU[:], in_=argU[:], pattern=[[0, 8], [1, 64]], compare_op=ALU.is_ge,
                                                       fill=k.fillreg(e, -30000.0), base=0, channel_multiplier=-1), [("argU",)], [("argU",)])
                k.act(decU[:], argU[:], AF.Exp)
                k.op("pool", lambda e: e.affine_select(out=decUs[:], in_=decU[:], pattern=[[0, 8], [1, 64]], compare_op=ALU.is_gt,
                                                       fill=k.fillreg(e, 0.0), base=0, channel_multiplier=-1), [("decU",)], [("decUs",)])
                k.tt(argL[:], GTrep, r3(pG[0:64, :]), ALU.subtract, rk=[("colT",), _key(pG[:])])
                k.op("pool", lambda e: e.affine_select(out=argL[:], in_=argL[:], pattern=[[0, 8], [-1, 64]], compare_op=ALU.is_gt,
                                                       fill=k.fillreg(e, -30000.0), base=0, channel_multiplier=1), [("argL",)], [("argL",)])
                k.act(decLs[:], argL[:], AF.Exp)
                k.tt(qg[:], cs_[0][:], egbc[:], ALU.mult)
                k.tt(KBb[:], cs_[1][:], pB[:, :], ALU.mult)
                pA = nps(); pMT = nps(); pM = nps()
                for ci in range(8):
                    cl = slice(ci * 64, ci * 64 + 64)
                    k.mm(pA[0:64, cl], Kb[:, cl], Qb[:, cl])
                    k.mm(pMT[0:64, cl], Kb[:, cl], KBb[:, cl])
                    k.mm(pM[0:64, cl], KBb[:, cl], Kb[:, cl])
                k.tt(attnT[:], pA[0:64, :], decU[:].rearrange("p c j -> p (c j)"), ALU.mult)
                k.stt(PTb[0][:], pMT[0:64, :], -1.0, decUs[:].rearrange("p c j -> p (c j)"), ALU.mult, ALU.mult)
                k.stt(Pb[0][:], pM[0:64, :], -1.0, decLs[:].rearrange("p c j -> p (c j)"), ALU.mult, ALU.mult)
                k.stt(TT32[:], pMT[0:64, :], -1.0, decUs[:].rearrange("p c j -> p (c j)"), ALU.mult, ALU.mult)
                k.tt(TT32[:], TT32[:], identrep[:].rearrange("p c j -> p (c j)"), ALU.add)
                k.copy(TTb[:], TT32[:], eng="pool")
                cur = 0
                for lv in range(1, 6):
                    nxt = cur ^ 1
                    pP = nps()
                    for ci in range(8):
                        cl = slice(ci * 64, ci * 64 + 64)
                        k.mm(pP[0:64, cl], PTb[cur][:, cl], Pb[cur][:, cl])
                    k.act(Pb[nxt][:], pP[0:64, :], AF.Copy)
                    if lv < 5:
                        pPT = nps()
                        for ci in range(8):
                            cl = slice(ci * 64, ci * 64 + 64)
                            k.mm(pPT[0:64, cl], Pb[cur][:, cl], PTb[cur][:, cl])
                        k.act(PTb[nxt][:], pPT[0:64, :], AF.Copy)
                    pT_ = nps()
                    for ci in range(8):
                        cl = slice(ci * 64, ci * 64 + 64)
                        k.mm(pT_[0:64, cl], Pb[nxt][:, cl], TTb[:, cl])
                    k.tt(TT32[:], TT32[:], pT_[0:64, :], ALU.add)
                    k.copy(TTb[:], TT32[:], eng="pool")
                    cur = nxt
                for hf in range(2):
                    pK = nps(); pV = nps()
                    for cj in range(4):
                        ci = hf * 4 + cj
                        cl = slice(ci * 64, ci * 64 + 64)
                        k.mm(pK[0:64, cj * 128:(cj + 1) * 128], Kb[:, cl], identb[:])
                        k.mm(pV[0:64, cj * 128:(cj + 1) * 128], Vb[:, cl], identb[:])
                    cc0 = c0 + hf * 4
                    S1rep = colT[:, 2, cc0:cc0 + 4].unsqueeze(2).broadcast_to([64, 4, 128])
                    S3rep = colT[:, 3, cc0:cc0 + 4].unsqueeze(2).broadcast_to([64, 4, 128])
                    Brep = colT[:, 1, cc0:cc0 + 4].unsqueeze(2).broadcast_to([64, 4, 128])
                    pK3 = pK[0:64, :].rearrange("p (c d) -> p c d", c=4)
                    pV3 = pV[0:64, :].rearrange("p (c d) -> p c d", c=4)
                    k.tt(kbg[:, hf * 4:hf * 4 + 4, :], pK3, S1rep, ALU.mult, wk=[("kbg", hf)])
                    k.tt(kg[:, hf * 4:hf * 4 + 4, :], pK3, S3rep, ALU.mult, wk=[("kg", hf)])
                    k.tt(vbt[:, hf * 4:hf * 4 + 4, :], pV3, Brep, ALU.mult, wk=[("vbt", hf)])
                pW = nps()
                for ci in range(8):
                    cl = slice(ci * 64, ci * 64 + 64)
                    k.mm(pW[:, cl], kbg[:, ci, :], TTb[:, cl])
                k.act(wTb[:], pW[:, :], AF.Copy)
                for hf in range(2):
                    pU = nps()
                    for cj in range(4):
                        ci = hf * 4 + cj
                        cl = slice(ci * 64, ci * 64 + 64)
                        k.mm(pU[0:64, cj * 128:(cj + 1) * 128], TTb[:, cl], vbt[:, ci, :])
                    k.act(u32[:, hf * 4:hf * 4 + 4, :].rearrange("p c d -> p (c d)"), pU[0:64, :], AF.Copy, wk=[("u32", hf)])
                for ci in range(8):
                    cl = slice(ci * 64, ci * 64 + 64)
                    vn_ = vnew[ci % 2]
                    pR = nps()
                    k.mm(pR[0:64, 0:128], wTb[:, cl], Sb[:])
                    k.tt(vn_[:], u32[:, ci, :], pR[0:64, 0:128], ALU.subtract)
                    k.mm(psO[:, cl], Sb[:], qg[:, cl], start=True, stop=False)
                    k.mm(psO[:, cl], vn_[:], attnT[:, cl], start=False, stop=True)
                    k.mm(pR[:, 128:256], kg[:, ci, :], vn_[:])
                    k.stt(S32[:], S32[:], egbc[:, ci * 64 + 63:ci * 64 + 64], pR[:, 128:256], ALU.mult, ALU.add)
                    k.act(Sb[:], S32[:], AF.Copy)
                k.act(o32[:], psO[:, :], AF.Copy)
                r = c.rstd_of([o32[:]], GT, 128.0, out=rq)
                y_ = yat[gi % 2]
                k.stt(y_[:], o32[:], gp[:, 12:13], r[:], ALU.mult, ALU.mult)
                k.act(z_[:], z_[:], AF.Silu)
                k.tt(y_[:], y_[:], z_[:], ALU.mult)
                k.dma(Yf(j)[:, sl], y_[:])


    k.prefix = base_prefix + "sg_"
    if "s" in B_PARTS:
        with k.scope():
            sgu_part()
        k.barrier()
    for j in range(4 if "h" in B_PARTS else 0):
        k.prefix = base_prefix + "h%d_" % j
        with k.scope():
            head(j)
        k.barrier()
    k.prefix = base_prefix


def build_fused(L, depth):
    nc = bass.Bass("TRN2", target_bir_lowering=False)
    c = Ctx(nc); k = c.k
    xT = c.dram("xT", [128, 16, L])
    xo = c.dram("xo", [128, 16, L], kind="ExternalOutput")
    PT = nc.dram_tensor("PT", [44, 128, L], F32)
    PBA = nc.dram_tensor("PBA", [8, L], F32)
    Y = nc.dram_tensor("Y", [16, 128, L], F32)
    PTa = PT.ap(); PBAa = PBA.ap(); Ya = Y.ap()
    uv_view = PTa[24:32].rearrange("c p t -> p c t")
    Y_view = Ya.rearrange("c p t -> p c t")
    wsrc = {}
    pending = []
    for l in range(depth):
        sfx = "_%d" % l
        for nm, shp in (("w", [11, 128, 16, 512]), ("wout", [8, 128, 16, 256]), ("wup", [44, 128, 16, 256]),
                        ("wdn", [16, 128, 44, 128])):
            src = c.dram(nm + sfx, shp)
            dstt = nc.dram_tensor(nm + "b" + sfx, shp, BF16)
            k.sbnames.add(dstt.ap().name)
            wsrc[nm + sfx] = (src, dstt.ap())
            pending.append((src, dstt.ap(), shp))
    k.prefix = "PRO_"
    with k.scope():
        stg = [k.sb([128, 8192], BF16, "stg%d" % i) for i in range(4)]
        si = 0
        for src, dst, shp in pending[:1]:
            per = shp[2] * shp[3]
            for i in range(shp[0]):
                t_ = stg[si % 4]; si += 1
                k.dma(t_[:, :per], src[i].rearrange("p a b -> p (a b)"), eng="pool")
                k.dma(dst[i].rearrange("p a b -> p (a b)"), t_[:, :per], wk=[(dst.name, i)])
    k.barrier(include_pool=True)

    def cast_rest():
        stg2 = [k.sb([128, 8192], BF16, "stgb%d" % i) for i in range(2)]
        sj = 0
        for src, dst, shp in pending[1:]:
            per = shp[2] * shp[3]
            for i in range(shp[0]):
                t_ = stg2[sj % 2]; sj += 1
                k.dma(t_[:, :per], src[i].rearrange("p a b -> p (a b)"), eng="pool")
                k.dma(dst[i].rearrange("p a b -> p (a b)"), t_[:, :per], eng="pool", wk=[(dst.name, i)])
    for l in range(depth):
        sfx = "_%d" % l
        gam = c.dram("gam" + sfx, [128, 16]); w = wsrc["w" + sfx][1]; wba = c.dram("wba" + sfx, [128, 16, 8])
        gpar = c.dram("gpar" + sfx, [4, 128, 16]); lpar = c.dram("lpar" + sfx, [4, 128, 8]); lw = c.dram("lw" + sfx, [4, 2, 128, 128])
        spar = c.dram("spar" + sfx, [128, 8]); swT = c.dram("swT" + sfx, [128, 4, 128]); sbs = c.dram("sbs" + sfx, [128, 4, 128])
        scw = c.dram("scw" + sfx, [4, 128, 4])
        vecs = c.dram("vecs" + sfx, [128, 60]); wout = wsrc["wout" + sfx][1]
        wup = wsrc["wup" + sfx][1]; cw = c.dram("cw" + sfx, [128, 88, 4]); wdn = wsrc["wdn" + sfx][1]
        xin = xT if l == 0 else xo
        k.prefix = "L%dA_" % l
        with k.scope():
            emit_A(c, L, xin, gam, w, wba, lambda ch: PTa[ch], PBAa, w_eng="sp", wkey=w.name,
                   hook=(cast_rest if l == 0 else None))
        k.barrier(include_pool=(l == 0))
        k.prefix = "L%dB_" % l
        with k.scope():
            emit_B(c, L, lambda ch: PTa[ch], lambda r: PBAa[r], lambda sl: uv_view[:, :, sl], lambda ch: Ya[ch],
                   gpar, lpar, lw, spar, swT, sbs, scw)
        k.barrier()
        k.prefix = "L%dC_" % l
        with k.scope():
            emit_C(c, L, lambda c0, T: xin[:, :, c0:c0 + T], lambda g, c0, T: Y_view[:, 4 * g:4 * g + 4, c0:c0 + T],
                   lambda c0, T: xo[:, :, c0:c0 + T], vecs, wout, wup, cw, wdn, w_eng="sp",
                   wkeys=(wout.name, wup.name, wdn.name))
        k.barrier()
    k.prefix = ""
    k.finish()
    k.build()
    return nc


_PROGS = {}
_PERM = np.concatenate([np.arange(0, 2048), np.arange(2056, 5640)])


def _fm(v):
    return np.ascontiguousarray(v.reshape(-1, 128).T)


def host_weights(P, l):
    d = {}
    sfx = "_%d" % l
    W = P["w_in"][l]
    d["w"] = np.ascontiguousarray(W[:, _PERM].reshape(16, 128, 11, 512).transpose(2, 1, 0, 3))
    d["wba"] = np.ascontiguousarray(W[:, 2048:2056].reshape(16, 128, 8).transpose(1, 0, 2))
    d["gam"] = _fm(P["pre_mix_norm"][l])
    gpar = np.zeros((4, 128, 16), np.float32)
    lpar = np.zeros((4, 128, 8), np.float32)
    lw = np.zeros((4, 2, 128, 128), np.float32)
    scw = np.zeros((4, 128, 4), np.float32)
    for j in range(4):
        cs = slice(j * 128, (j + 1) * 128)
        for wq in range(3):
            gpar[j, :, 4 * wq:4 * wq + 4] = P["gdn_conv_w"][l][:, wq * 512 + j * 128: wq * 512 + (j + 1) * 128].T
        gpar[j, :, 12] = P["gdn_norm_w"][l]
        gpar[j, :, 13] = P["gdn_a_log"][l][j]
        gpar[j, :, 14] = P["gdn_dt_bias"][l][j]
        lpar[j, :, 0:4] = P["lru_conv_w"][l][:, cs].T
        lpar[j, :, 4] = P["lru_conv_b"][l][cs]
        lpar[j, :, 5] = P["lru_ba"][l].reshape(-1)[cs]
        lpar[j, :, 6] = P["lru_bx"][l].reshape(-1)[cs]
        lpar[j, :, 7] = P["lru_lambda"][l][cs]
        for q, nm in enumerate(("lru_wa", "lru_wx")):
            lw[j, q, 0:64, 0:64] = P[nm][l][2 * j]
            lw[j, q, 64:128, 64:128] = P[nm][l][2 * j + 1]
        scw[j, :, 0:3] = P["sconv_w"][l][:, cs].T
    d["gpar"] = gpar; d["lpar"] = lpar; d["lw"] = lw; d["scw"] = scw
    d["spar"] = np.concatenate([_fm(P["sgu_ln_w"][l]), _fm(P["sgu_ln_b"][l])], axis=1)
    d["swT"] = np.ascontiguousarray(P["sgu_ws"][l].transpose(2, 0, 1))
    d["sbs"] = np.ascontiguousarray(np.broadcast_to(P["sgu_b"][l][None], (128, 4, 128)))
    vecs = np.zeros((128, 60), np.float32)
    for g in range(3):
        vecs[:, 4 * g:4 * g + 4] = _fm(P["grp_norm_w"][l][g])
    vecs[:, 12:28] = _fm(P["post_mix_norm"][l])
    vecs[:, 28:44] = _fm(P["pre_ffn_norm"][l])
    vecs[:, 44:60] = _fm(P["post_ffn_norm"][l])
    d["vecs"] = vecs
    d["wout"] = np.ascontiguousarray(P["w_out"][l].reshape(16, 128, 8, 256).transpose(2, 1, 0, 3))
    up = P["ffn_up"][l]
    wg = up[:, :5632].reshape(16, 128, 44, 128)
    wv = up[:, 5632:].reshape(16, 128, 44, 128)
    d["wup"] = np.ascontiguousarray(np.stack([wg, wv], axis=3).transpose(2, 1, 0, 3, 4).reshape(44, 128, 16, 256))
    cw = np.zeros((128, 88, 4), np.float32)
    cwf = P["ffn_conv_w"][l]
    cbf = P["ffn_conv_b"][l]
    for gv in range(2):
        blk = cwf[:, gv * 5632:(gv + 1) * 5632].reshape(3, 44, 128)
        cw[:, gv::2, 0:3] = blk.transpose(2, 1, 0)
        cw[:, gv::2, 3] = cbf[gv * 5632:(gv + 1) * 5632].reshape(44, 128).T
    d["cw"] = cw
    d["wdn"] = np.ascontiguousarray(P["ffn_down"][l].reshape(44, 128, 16, 128).transpose(2, 1, 0, 3))
    return {kk + sfx: v for kk, v in d.items()}


def kernel(**inputs):
    x = np.asarray(inputs["x"], np.float32)
    P = {kk: np.asarray(v, np.float32) for kk, v in inputs.items() if kk != "x"}
    B, S, _ = x.shape
    depth = P["w_in"].shape[0]
    key = (S, depth)
    if key not in _PROGS:
        _PROGS[key] = build_fused(S, depth)
    nc = _PROGS[key]
    wts = {}
    for l in range(depth):
        wts.update(host_weights(P, l))
    maps = []
    for b in range(B):
        m = dict(wts)
        m["xT"] = np.ascontiguousarray(x[b].reshape(S, 16, 128).transpose(2, 1, 0))
        maps.append(m)
    res = run_bass_kernel_spmd(nc, maps, core_ids=list(range(B))).results
    out = np.empty((B, S, 2048), np.float32)
    for b in range(B):
        out[b] = res[b]["xo"].transpose(2, 1, 0).reshape(S, 2048)
    return out
```
